# Optimizing a Trainium2 kernel written in Bass

```python
import math
import jax, jax.numpy as jnp
from jax import lax
import numpy as np

D_MODEL = 1024
BATCH = 2
SEQ = 8192
DEPTH = 2
DEC_BATCH = 8
DEC_SEQ = 2048
PAST_LEN = 128

N_MIXERS = 2
N_HEADS = 16
HEAD_DIM = D_MODEL // N_HEADS
N_KV_HEADS = 4
Q_PER_KV = N_HEADS // N_KV_HEADS
ROT_HALF = HEAD_DIM // 2
ROPE_THETA = 10000.0
Q_BLOCK = 128
GRID_W = 64
SGU_CHUNK = 128
SGU_INNER = 2 * D_MODEL
SGU_GROUPS = 8
SGU_GROUP_DIM = SGU_INNER // SGU_GROUPS
FFN_DIM = 2816
CONV_W = 3
EPS = 1e-6

kernel_name = "hybrid_attn_sgu_convffn_encoder"


def rmsnorm(x, g):
    xf = x.astype(jnp.float32)
    y = xf * lax.rsqrt(jnp.mean(xf * xf, axis=-1, keepdims=True) + EPS)
    return (y * g.astype(jnp.float32)).astype(x.dtype)


def rotate_half(y):
    a, b = jnp.split(y, 2, axis=-1)
    return jnp.concatenate([-b, a], axis=-1)


def axial_rope_tables(rows):
    row_idx = jnp.broadcast_to(jnp.arange(rows)[:, None], (rows, GRID_W)).reshape(-1)
    col_idx = jnp.broadcast_to(jnp.arange(GRID_W)[None, :], (rows, GRID_W)).reshape(-1)
    inv_freq = ROPE_THETA ** (-jnp.arange(0, ROT_HALF, 2, dtype=jnp.float32) / ROT_HALF)
    ang_r = row_idx.astype(jnp.float32)[:, None] * inv_freq[None, :]
    ang_c = col_idx.astype(jnp.float32)[:, None] * inv_freq[None, :]
    ang = jnp.concatenate([ang_r, ang_r, ang_c, ang_c], axis=-1)
    return jnp.cos(ang), jnp.sin(ang)


def apply_axial_rope(x, cos, sin):
    xf = x.astype(jnp.float32)
    xr = jnp.concatenate([rotate_half(xf[..., :ROT_HALF]), rotate_half(xf[..., ROT_HALF:])], axis=-1)
    out = xf * cos[None, :, None, :] + xr * sin[None, :, None, :]
    return out.astype(x.dtype)


def attention_mixer(x, w_qkv, q_gain, k_gain, w_o):
    B, T, _ = x.shape
    rows = T // GRID_W
    cos, sin = axial_rope_tables(rows)
    qkv = x @ w_qkv
    q = qkv[..., :N_HEADS * HEAD_DIM].reshape(B, T, N_HEADS, HEAD_DIM)
    k = qkv[..., N_HEADS * HEAD_DIM:(N_HEADS + N_KV_HEADS) * HEAD_DIM].reshape(B, T, N_KV_HEADS, HEAD_DIM)
    v = qkv[..., (N_HEADS + N_KV_HEADS) * HEAD_DIM:].reshape(B, T, N_KV_HEADS, HEAD_DIM)
    q = apply_axial_rope(rmsnorm(q, q_gain), cos, sin)
    k = apply_axial_rope(rmsnorm(k, k_gain), cos, sin)
    scale = 1.0 / math.sqrt(HEAD_DIM)
    nb = T // Q_BLOCK
    qb = q.reshape(B, nb, Q_BLOCK, N_KV_HEADS, Q_PER_KV, HEAD_DIM).transpose(1, 0, 2, 3, 4, 5)

    def block(qi):
        s = jnp.einsum('bqkgd,bskd->bkgqs', qi, k).astype(jnp.float32) * scale
        p = jax.nn.softmax(s, axis=-1).astype(v.dtype)
        return jnp.einsum('bkgqs,bskd->bqkgd', p, v)

    o = lax.map(block, qb)
    o = o.transpose(1, 0, 2, 3, 4, 5).reshape(B, T, N_HEADS * HEAD_DIM)
    return o @ w_o


def sgu_mixer(x, w_in, v_gain, w_s, b_s, w_out):
    B, T, _ = x.shape
    nc = T // SGU_CHUNK
    z = jax.nn.gelu(x @ w_in)
    u, v = jnp.split(z, 2, axis=-1)
    v = rmsnorm(v, v_gain)
    vc = v.reshape(B, nc, SGU_CHUNK, SGU_GROUPS, SGU_GROUP_DIM)
    s = jnp.einsum('gpq,bcqgd->bcpgd', w_s, vc) + b_s.T[None, None, :, :, None]
    y = u * s.reshape(B, T, SGU_INNER)
    return y @ w_out


def conv_ffn(x, w_up, conv_w, conv_b, w_down):
    h = x @ w_up
    hp = jnp.pad(h, ((0, 0), (1, 1), (0, 0)))
    h = hp[:, :-2] * conv_w[0] + hp[:, 1:-1] * conv_w[1] + hp[:, 2:] * conv_w[2] + conv_b
    gate, up = jnp.split(h, 2, axis=-1)
    return (jax.nn.silu(gate) * up) @ w_down


def trunk(x, norm_mix, norm_ffn, attn_w_qkv, attn_q_norm, attn_k_norm, attn_w_o,
          sgu_w_in, sgu_v_norm, sgu_w_s, sgu_b_s, sgu_w_out,
          ffn_w_up, ffn_conv_w, ffn_conv_b, ffn_w_down):
    for i in range(DEPTH):
        h = rmsnorm(x, norm_mix[i])
        j = i // N_MIXERS
        if i % N_MIXERS == 0:
            mix = attention_mixer(h, attn_w_qkv[j], attn_q_norm[j], attn_k_norm[j], attn_w_o[j])
        else:
            mix = sgu_mixer(h, sgu_w_in[j], sgu_v_norm[j], sgu_w_s[j], sgu_b_s[j], sgu_w_out[j])
        x = x + mix
        h = rmsnorm(x, norm_ffn[i])
        x = x + conv_ffn(h, ffn_w_up[i], ffn_conv_w[i], ffn_conv_b[i], ffn_w_down[i])
    return x


def setup_inputs(seed: int = 0) -> dict:
    key = jax.random.key(seed)
    ks = jax.random.split(key, 20)
    n_a = (DEPTH + N_MIXERS - 1) // N_MIXERS
    n_b = DEPTH // N_MIXERS
    f32 = jnp.float32

    def w(k, shape, fan_in):
        return jax.random.normal(k, shape, f32) * (fan_in ** -0.5)

    def gain(k, shape):
        return 1.0 + 0.02 * jax.random.normal(k, shape, f32)

    qkv_out = (N_HEADS + 2 * N_KV_HEADS) * HEAD_DIM
    return {
        "x_prompt": jax.random.normal(ks[0], (BATCH, SEQ, D_MODEL), f32),
        "x_sample": jax.random.normal(ks[1], (DEC_BATCH, DEC_SEQ, D_MODEL), f32),
        "norm_mix": gain(ks[2], (DEPTH, D_MODEL)),
        "norm_ffn": gain(ks[3], (DEPTH, D_MODEL)),
        "attn_w_qkv": w(ks[4], (n_a, D_MODEL, qkv_out), D_MODEL),
        "attn_q_norm": gain(ks[5], (n_a, HEAD_DIM)),
        "attn_k_norm": gain(ks[6], (n_a, HEAD_DIM)),
        "attn_w_o": w(ks[7], (n_a, N_HEADS * HEAD_DIM, D_MODEL), N_HEADS * HEAD_DIM),
        "sgu_w_in": w(ks[8], (n_b, D_MODEL, 2 * SGU_INNER), D_MODEL),
        "sgu_v_norm": gain(ks[9], (n_b, SGU_INNER)),
        "sgu_w_s": w(ks[10], (n_b, SGU_GROUPS, SGU_CHUNK, SGU_CHUNK), SGU_CHUNK),
        "sgu_b_s": gain(ks[11], (n_b, SGU_GROUPS, SGU_CHUNK)),
        "sgu_w_out": w(ks[12], (n_b, SGU_INNER, D_MODEL), SGU_INNER),
        "ffn_w_up": w(ks[13], (DEPTH, D_MODEL, 2 * FFN_DIM), D_MODEL),
        "ffn_conv_w": w(ks[14], (DEPTH, CONV_W, 2 * FFN_DIM), CONV_W),
        "ffn_conv_b": 0.02 * jax.random.normal(ks[15], (DEPTH, 2 * FFN_DIM), f32),
        "ffn_w_down": w(ks[16], (DEPTH, FFN_DIM, D_MODEL), FFN_DIM),
    }


def reference(x_prompt, x_sample, norm_mix, norm_ffn, attn_w_qkv, attn_q_norm, attn_k_norm, attn_w_o,
              sgu_w_in, sgu_v_norm, sgu_w_s, sgu_b_s, sgu_w_out,
              ffn_w_up, ffn_conv_w, ffn_conv_b, ffn_w_down):
    y_prompt = trunk(x_prompt, norm_mix, norm_ffn, attn_w_qkv, attn_q_norm, attn_k_norm, attn_w_o,
                     sgu_w_in, sgu_v_norm, sgu_w_s, sgu_b_s, sgu_w_out,
                     ffn_w_up, ffn_conv_w, ffn_conv_b, ffn_w_down)
    y_sample = trunk(x_sample, norm_mix, norm_ffn, attn_w_qkv, attn_q_norm, attn_k_norm, attn_w_o,
                     sgu_w_in, sgu_v_norm, sgu_w_s, sgu_b_s, sgu_w_out,
                     ffn_w_up, ffn_conv_w, ffn_conv_b, ffn_w_down)
    return (y_prompt, y_sample)
```

```python
import math
import os
CUT = int(os.environ.get('CUT', '99'))
STAGE_W = os.environ.get('STAGE_W', '1') == '1'
from contextlib import ExitStack

import numpy as np
import concourse.bass as bass
import concourse.mybir as mybir
from concourse.bass_utils import run_bass_kernel_spmd

F32 = mybir.dt.float32
BF16 = mybir.dt.bfloat16
AF = mybir.ActivationFunctionType
ALU = mybir.AluOpType
AX = mybir.AxisListType

D = 1024
NH, HD, NKV = 16, 64, 4
FF = 2816
SEQ_P, SEQ_S = 8192, 2048
WIN = 2560
HALO = 256
EPS = 1e-6
N_DMA_SEMS = {"sp": 12, "pool": 6}


class Buf:
    __slots__ = ("name", "last_w", "readers", "dreaders")

    def __init__(self, name):
        self.name = name
        self.last_w = None
        self.readers = {}
        self.dreaders = []


class Rec:
    ENGS = ("pe", "act", "dve", "pool", "sp")

    def __init__(self, nc):
        self.nc = nc
        self.ops = {e: [] for e in self.ENGS}
        self.dma_rr = {e: 0 for e in self.ENGS}
        self.dma_last = {}
        self.last_compute = {}
        self.out_dmas = []

    def op(self, eng, fn, reads=(), writes=(), dma=False, out=False):
        ops = self.ops
        idx = len(ops[eng])
        key = (eng, idx)
        deps = set()
        for b in reads:
            w = b.last_w
            if w is not None:
                if w[0] != eng or eng != "pe" or ops[w[0]][w[1]]["dma"]:
                    deps.add(w)
        for b in writes:
            w = b.last_w
            if w is not None and (w[0] != eng or eng != "pe" or ops[w[0]][w[1]]["dma"] or dma):
                deps.add(w)
            for re_, ri in b.readers.items():
                if re_ != eng or eng != "pe" or dma:
                    deps.add((re_, ri))
            for r in b.dreaders:
                deps.add(r)
        slot = None
        if dma:
            n = N_DMA_SEMS[eng]
            slot = self.dma_rr[eng] % n
            self.dma_rr[eng] += 1
            prev = self.dma_last.get((eng, slot))
            if prev is not None:
                deps.add(prev)
            self.dma_last[(eng, slot)] = key
        else:
            self.last_compute[eng] = key
        ops[eng].append(dict(fn=fn, deps=deps, dma=dma, slot=slot, sig=False, tok=None))
        for b in reads:
            if dma:
                b.dreaders.append(key)
            else:
                b.readers[eng] = idx
        for b in writes:
            b.last_w = key
            b.readers = {}
            b.dreaders = []
        if out:
            self.out_dmas.append(key)
        return key

    def barrier(self):
        allk = set(self.last_compute.values()) | set(self.dma_last.values())
        for e in self.ENGS:
            deps = set(k for k in allk if not (k[0] == e and not self.ops[k[0]][k[1]]["dma"]))
            self.ops[e].append(dict(fn=None, deps=deps, dma=False, slot=None, sig=False, tok=None))

    def emit(self, es):
        nc = self.nc
        fin = set(self.out_dmas) | set(self.dma_last.values())
        self.ops["sp"].append(dict(fn=None, deps=fin, dma=False, slot=None, sig=False, tok=None))
        for e in self.ENGS:
            for o in self.ops[e]:
                for (de, di) in o["deps"]:
                    self.ops[de][di]["sig"] = True
        csem = {e: es.enter_context(nc.semaphore("c_" + e)) for e in self.ENGS}
        dsem = {e: [es.enter_context(nc.semaphore("d_%s%d" % (e, i))) for i in range(N_DMA_SEMS[e])]
                for e in N_DMA_SEMS}
        for e in self.ENGS:
            cnt = 0
            dcnt = [0] * N_DMA_SEMS.get(e, 0)
            for o in self.ops[e]:
                if o["dma"]:
                    dcnt[o["slot"]] += 16
                    o["tok"] = (("d", e, o["slot"]), dcnt[o["slot"]])
                elif o["sig"]:
                    cnt += 1
                    o["tok"] = (("c", e), cnt)

        def semof(k):
            return csem[k[1]] if k[0] == "c" else dsem[k[1]][k[2]]

        engobj = {"pe": "tensor", "act": "scalar", "dve": "vector", "pool": "gpsimd", "sp": "sync"}
        stats = {}
        allops = self.ops
        with nc.Block() as block:
            for e in self.ENGS:
                def body(eng, ops=allops[e], e=e):
                    waited = {}
                    nw = 0
                    for o in ops:
                        need = {}
                        for (de, di) in o["deps"]:
                            k, v = allops[de][di]["tok"]
                            if waited.get(k, 0) < v and need.get(k, 0) < v:
                                need[k] = v
                        for k, v in need.items():
                            eng.wait_ge(semof(k), v)
                            waited[k] = v
                            nw += 1
                        if o["fn"] is None:
                            continue
                        ins = o["fn"](eng)
                        if o["dma"]:
                            ins.then_inc(semof(o["tok"][0]), 16)
                        elif o["sig"]:
                            ins.then_inc(semof(o["tok"][0]), 1)
                    stats[e] = (len(ops), nw)

                getattr(block, engobj[e])(body)
        return stats


class T:
    __slots__ = ("v", "b", "extra")

    def __init__(self, v, name):
        self.v = v
        self.b = Buf(name)
        self.extra = []


class Arena:
    def __init__(self, nc, nbytes):
        self.t = nc.alloc_sbuf_tensor("arena", [128, nbytes // 2], BF16)
        self.top = 0
        self.cap = nbytes
        self.n = 0

    def mark(self):
        return self.top

    def reset(self, m):
        self.top = m

    def alloc(self, shape, dtype, name=None):
        esz = 4 if dtype == F32 else 2
        ne = 1
        for s in shape[1:]:
            ne *= s
        nb = (ne * esz + 63) // 64 * 64
        off = self.top
        self.top += nb
        assert self.top <= self.cap, ("SBUF arena overflow", name, self.top, self.cap)
        v = self.t[:, off // 2: off // 2 + ne * esz // 2]
        if dtype == F32:
            v = v.bitcast(F32)
        if len(shape) > 2:
            names = "abcde"[: len(shape) - 1]
            kw = {names[i]: shape[i + 1] for i in range(len(shape) - 1)}
            v = v.rearrange("p (%s) -> p %s" % (" ".join(names), " ".join(names)), **kw)
        self.n += 1
        return T(v, name or ("t%d" % self.n))


def head_of_slot(s):
    pi, par = s // 2, s % 2
    return 8 * (pi // 4) + 4 * par + (pi % 4)


def split(n, m):
    k = (n + m - 1) // m
    base = (n + k - 1) // k
    out = []
    s = 0
    while s < n:
        sz = min(base, n - s)
        out.append((s, sz))
        s += sz
    return out


def build(stop_after=None):
    nc = bass.Bass("TRN2", target_bir_lowering=False)

    def din(name, shape):
        return nc.dram_tensor(name, list(shape), F32, kind="ExternalInput").ap()

    xp_full = din("xp_full", [SEQ_P, D])
    xp_win = din("xp_win", [WIN, D])
    xs_in = din("xs", [SEQ_S, D])
    rope_seq = din("rope_seq", [SEQ_P, 128])
    rope_win = din("rope_win", [WIN, 128])
    mask_in = din("mask", [128, 2])
    w_qkv = din("w_qkv", [D, 1536])
    w_o = din("w_o", [D, D])
    w_in = din("w_in", [D, 4096])
    w_out = din("w_out", [2048, D])
    ws_T = din("ws_T", [128, 8 * 128])
    w_up = din("w_up", [2, D, 2 * FF])
    w_dn = din("w_dn", [2, FF, D])
    gcols_in = din("gcols", [128, 32])
    gqk_in = din("gqk", [256])
    vgain_in = din("vgain", [2048])
    bsT_in = din("bsT", [2048])
    convw_in = din("convw", [2, 128, 132])
    convb_in = din("convb", [2, 128, 44])
    y_p = nc.dram_tensor("y_p", [2048, D], F32, kind="ExternalOutput").ap()
    y_s = nc.dram_tensor("y_s", [2048, D], F32, kind="ExternalOutput").ap()
    NROW = WIN + SEQ_S
    X1 = nc.dram_tensor("X1", [NROW, D], F32).ap()
    X2 = nc.dram_tensor("X2", [NROW, D], F32).ap()
    X3 = nc.dram_tensor("X3", [NROW, D], F32).ap()
    SEG = {"P": (0, WIN), "S": (WIN, SEQ_S)}
    WUPB = [nc.dram_tensor("WUPB%d" % l, [D, 2 * FF], BF16).ap() for l in range(2)]
    WDNB = [nc.dram_tensor("WDNB%d" % l, [FF, D], BF16).ap() for l in range(2)]
    WINB = nc.dram_tensor("WINB", [D, 4096], BF16).ap()
    WOUTB = nc.dram_tensor("WOUTB", [2048, D], BF16).ap()
    QS = nc.dram_tensor("QS", [SEQ_S // 128, 128, 1024], BF16).ap()

    R = Rec(nc)
    ar = Arena(nc, 211968)
    psum = nc.alloc_psum_tensor("psum", [128, 8, 512], F32)
    PB = [T(psum[:, i, :], "bank%d" % i) for i in range(8)]

    def pview(i, n=1):
        return psum[:, i:i + n, :]

    def bf16_bank(i):
        return psum[:, i, :].bitcast(BF16).rearrange("p (k t) -> p k t", k=8)

    def bufs(ts):
        out = []
        for t in ts:
            if isinstance(t, T):
                out.append(t.b)
                out.extend(t.extra)
            else:
                out.append(t)
        return out

    def load_rows(xt, src, r0, n):
        n16 = (n // 16) * 16
        if n16 == n or n16 == 0:
            R.op("sp", lambda e: e.dma_start(out=xt.v[0:n], in_=src[r0:r0 + n, :]), [], bufs([xt]), dma=True)
            return
        if not xt.extra:
            xt.extra.append(Buf("x_tail"))
        R.op("sp", lambda e: e.dma_start(out=xt.v[0:n16], in_=src[r0:r0 + n16, :]), [], [xt.b], dma=True)
        R.op("sp", lambda e: e.dma_start(out=xt.v[n16:n], in_=src[r0 + n16:r0 + n, :]), [], [xt.extra[0]], dma=True)

    def store_rows(dst, r0, xo, n, is_out=False):
        n16 = (n // 16) * 16
        parts = [(0, n)] if (n16 == n or n16 == 0) else [(0, n16), (n16, n)]
        for (a, b_) in parts:
            R.op("sp", lambda e, a=a, b_=b_: e.dma_start(out=dst[r0 + a:r0 + b_, :], in_=xo.v[a:b_]),
                 bufs([xo]), [], dma=True, out=is_out)

    def ACT(out, in_, func, reads, writes, **kw):
        R.op("act", lambda e: e.activation(out=out, in_=in_, func=func, **kw), bufs(reads), bufs(writes))

    def TT(eng, out, in0, in1, op, reads, writes):
        R.op(eng, lambda e: e.tensor_tensor(out=out, in0=in0, in1=in1, op=op), bufs(reads), bufs(writes))

    def STT(eng, out, in0, scalar, in1, op0, op1, reads, writes):
        R.op(eng, lambda e: e.scalar_tensor_tensor(out=out, in0=in0, scalar=scalar, in1=in1, op0=op0, op1=op1),
             bufs(reads), bufs(writes))

    def TS(eng, out, in0, s1, s2, op0, op1, reads, writes):
        if s2 is None:
            R.op(eng, lambda e: e.tensor_scalar(out=out, in0=in0, scalar1=s1, scalar2=None, op0=op0),
                 bufs(reads), bufs(writes))
        else:
            R.op(eng, lambda e: e.tensor_scalar(out=out, in0=in0, scalar1=s1, scalar2=s2, op0=op0, op1=op1),
                 bufs(reads), bufs(writes))

    def CP(eng, out, in_, reads, writes):
        R.op(eng, lambda e: e.tensor_copy(out=out, in_=in_), bufs(reads), bufs(writes))

    def MEMSET(eng, out, val, writes):
        R.op(eng, lambda e: e.memset(out, val), (), bufs(writes))

    def DMA(eng, out, in_, reads, writes, is_out=False):
        R.op(eng, lambda e: e.dma_start(out=out, in_=in_), bufs(reads), bufs(writes), dma=True, out=is_out)

    def PE(fn, reads, writes):
        R.op("pe", fn, bufs(reads), bufs(writes))

    ident = ar.alloc([128, 128], BF16, "ident")
    identf = ar.alloc([128, 128], F32, "identf")
    ones = ar.alloc([128, 64], F32, "ones")
    epsD = ar.alloc([128, 1], F32, "epsD")
    gcols = ar.alloc([128, 4, 8], F32, "gcols")
    gqk = ar.alloc([128, 256], F32, "gqk")
    maskt = ar.alloc([128, 2], F32, "mask")
    MEMSET("pool", identf.v, 0.0, [identf])
    R.op("pool", lambda e: e.affine_select(out=identf.v, in_=identf.v, pattern=[[-1, 128]],
                                           compare_op=ALU.not_equal, fill=1.0, base=0, channel_multiplier=1),
         [identf.b], [identf.b])
    CP("pool", ident.v, identf.v, [identf], [ident])
    MEMSET("pool", ones.v, 1.0, [ones])
    onesb = ar.alloc([128, 64], BF16, "onesb")
    MEMSET("pool", onesb.v, 1.0, [onesb])
    MEMSET("pool", epsD.v, EPS, [epsD])
    DMA("sp", gcols.v, gcols_in.rearrange("p (a b) -> p a b", a=4), [], [gcols])
    DMA("sp", gqk.v, gqk_in.partition_broadcast(128), [], [gqk])
    DMA("sp", maskt.v, mask_in, [], [maskt])
    pmark = ar.mark()

    class NormCtx:
        def __init__(self, nslots=2, tbanks=(0,)):
            self.slots = []
            for i in range(nslots):
                xsb = ar.alloc([128, D], BF16, "nxs%d" % i)
                self.slots.append(dict(junk=xsb, xsb=xsb, st=ar.alloc([128, 4], F32, "nst%d" % i)))
            self.i = 0
            self.tbanks = list(tbanks)
            self.scale_eng = "dve"
            self.rsqrt_eng = "act"

    def rsqrt_dve(st, n, scale):
        I32 = mybir.dt.int32
        x, y, t = st.v[0:n, 1:2], st.v[0:n, 2:3], st.v[0:n, 3:4]
        TS("dve", x, st.v[0:n, 0:1], scale, EPS, ALU.mult, ALU.add, [st], [st])
        TS("dve", y.bitcast(I32), x.bitcast(I32), 1, None, ALU.arith_shift_right, None, [st], [st])
        TS("dve", y.bitcast(I32), y.bitcast(I32), -1, 0x5f3759df, ALU.mult, ALU.add, [st], [st])
        for _ in range(3):
            STT("dve", t, y, y, x, ALU.mult, ALU.mult, [st], [st])
            TS("dve", t, t, -0.5, 1.5, ALU.mult, ALU.add, [st], [st])
            TT("dve", y, y, t, ALU.mult, [st], [st])

    def norm_T_ops(nctx, xt, n, gidx, out_ap, out_t):
        s = nctx.slots[nctx.i % len(nctx.slots)]
        tbank = nctx.tbanks[nctx.i % len(nctx.tbanks)]
        nctx.i += 1
        junk, xsb, st = s["junk"], s["xsb"], s["st"]
        tb = PB[tbank]
        pt = bf16_bank(tbank)
        scale_eng = nctx.scale_eng

        def o1():
            R.op("dve", lambda e: e.scalar_tensor_tensor(out=junk.v[0:n], in0=xt.v[0:n], scalar=1.0, in1=xt.v[0:n],
                                                         op0=ALU.mult, op1=ALU.mult, accum_out=st.v[0:n, 0:1]),
                 bufs([xt]), bufs([junk, st]))

        rs_eng = nctx.rsqrt_eng

        def o2():
            if rs_eng == "dve":
                rsqrt_dve(st, n, 1.0 / D)
            else:
                ACT(st.v[0:n, 1:2], st.v[0:n, 0:1], AF.Ln, [st, epsD], [st], scale=1.0 / D, bias=epsD.v[0:n])
                ACT(st.v[0:n, 2:3], st.v[0:n, 1:2], AF.Exp, [st], [st], scale=-0.5)

        def o3():
            if scale_eng == "act":
                ACT(xsb.v[0:n], xt.v[0:n], AF.Copy, [xt, st], [xsb], scale=st.v[0:n, 2:3])
            else:
                TS("dve", xsb.v[0:n], xt.v[0:n], st.v[0:n, 2:3], None, ALU.mult, None, [xt, st], [xsb])

        def o4():
            def tr(e):
                for k in range(8):
                    i = e.transpose(out=pt[:, k, 0:n], in_=xsb.v[0:n, k * 128:(k + 1) * 128], identity=ident.v[0:n, 0:n])
                return i
            PE(tr, [xsb, ident], [tb])

        def o5():
            TT("dve", out_ap, pt[:, :, 0:n], gcols.v[:, gidx, :].unsqueeze(2).broadcast_to([128, 8, n]), ALU.mult,
               [tb, gcols], [out_t])
        return [o1, o2, o3, o4, o5]

    def norm_T(nctx, xt, n, gidx, out_ap, out_t):
        for o in norm_T_ops(nctx, xt, n, gidx, out_ap, out_t):
            o()

    def qk_post_ops(src, H, ropet, goff, scr, out):
        sq, t1, AB, st = scr["sq"], scr["t1"], scr["AB"], scr["st"]
        W = H * 64
        s3 = src.v[:, 0:W].rearrange("p (h d) -> p h d", h=H)

        def o1():
            TT("pool", AB.v, ropet.v, gqk.v[:, goff:goff + 128], ALU.mult, [ropet, gqk], [AB])

        def o2():
            TT("dve", sq.v[:, 0:W], src.v[:, 0:W], src.v[:, 0:W], ALU.mult, [src], [sq])
            R.op("dve", lambda e: e.tensor_reduce(out=st.v[:, 0:H], in_=sq.v[:, 0:W].rearrange("p (h d) -> p h d", h=H),
                                                  axis=AX.X, op=ALU.add), [sq.b], [st.b])
            A_bc = AB.v[:, 0:64].unsqueeze(1).broadcast_to([128, H, 64])
            TT("dve", t1.v[:, 0:W].rearrange("p (h d) -> p h d", h=H), s3, A_bc, ALU.mult, [src, AB], [t1])

        def o3():
            ACT(st.v[:, 32:32 + H], st.v[:, 0:H], AF.Ln, [st, epsD], [st], scale=1.0 / HD, bias=epsD.v)
            ACT(st.v[:, 64:64 + H], st.v[:, 32:32 + H], AF.Exp, [st], [st], scale=-0.5)

        def o4():
            s5 = src.v[:, 0:W].rearrange("p (h a b c) -> p h a b c", h=H, a=2, b=2)
            q5 = sq.v[:, 0:W].rearrange("p (h a b c) -> p h a b c", h=H, a=2, b=2)
            B5 = AB.v[:, 64:128].rearrange("p (a b c) -> p a b c", a=2, b=2)
            for blk in range(2):
                TT("pool", q5[:, :, :, blk, :], s5[:, :, :, 1 - blk, :],
                   B5[:, :, blk, :].unsqueeze(1).broadcast_to([128, H, 2, 16]), ALU.mult, [src, AB, st], [sq])
            TT("pool", t1.v[:, 0:W], t1.v[:, 0:W], sq.v[:, 0:W], ALU.add, [t1, sq], [t1])

        def o5():
            TT("dve", out.v[:, 0:W].rearrange("p (h d) -> p h d", h=H),
               t1.v[:, 0:W].rearrange("p (h d) -> p h d", h=H),
               st.v[:, 64:64 + H].unsqueeze(2).broadcast_to([128, H, 64]), ALU.mult, [t1, st], [out])
        return [o1, o2, o3, o4, o5]

    def qk_post(src, H, ropet, goff, scr, out):
        for o in qk_post_ops(src, H, ropet, goff, scr, out):
            o()

    def pipeline(n_items, make_stages, starts=None):
        if starts is None:
            starts = list(range(n_items))
        live = {}
        nxt = 0
        it_ = 0
        while nxt < n_items or live:
            while nxt < n_items and starts[nxt] <= it_:
                live[nxt] = make_stages(nxt)
                nxt += 1
            for i in sorted(live):
                k = it_ - starts[i]
                if 0 <= k < len(live[i]):
                    live[i][k]()
            for i in [i for i in live if it_ - starts[i] >= len(live[i]) - 1]:
                del live[i]
            it_ += 1

    def interleave(dl, bl):
        if dl:
            dl[0]["load"]()
        if bl:
            bl[0]["load"]()
        for i in range(max(len(dl), len(bl))):
            if i + 1 < len(dl):
                dl[i + 1]["load"]()
            if i < len(bl):
                bl[i]["ops"][0]()
                bl[i]["ops"][1]()
                bl[i]["ops"][2]()
            if i + 1 < len(bl):
                bl[i + 1]["load"]()
            if i < len(dl):
                dl[i]["mm"]()
            if i < len(bl):
                bl[i]["ops"][3]()
            if i < len(dl):
                dl[i]["fin"]()
            if i < len(bl):
                bl[i]["ops"][4]()

    NKT = SEQ_P + SEQ_S
    kT = ar.alloc([128, 2, NKT], BF16, "kT")
    Vx = ar.alloc([128, NKT // 128, 4, 66], BF16, "Vx")
    MEMSET("pool", Vx.v[:, :, :, 64:65], 1.0, [Vx])
    wq = ar.alloc([128, 8, 1024], BF16, "wq")
    DMA("pool", wq.v, w_qkv[:, 0:1024].rearrange("(k p) f -> p k f", p=128), [], [wq])
    abmark = ar.mark()
    wkv = ar.alloc([128, 8, 512], BF16, "wkv")
    DMA("pool", wkv.v, w_qkv[:, 1024:1536].rearrange("(k p) f -> p k f", p=128), [], [wkv])
    nctx = NormCtx(4, tbanks=(0, 5))
    nctx.scale_eng = "act"
    xin = [ar.alloc([128, D], F32, "xin%d" % i) for i in range(4)]
    ropet = [ar.alloc([128, 128], F32, "ropet%d" % i) for i in range(12)]
    xnTa = [ar.alloc([128, 8, 128], BF16, "xnTa%d" % i) for i in range(3)]
    NK = 7
    kraws = [ar.alloc([128, 512], F32, "kraw%d" % i) for i in range(NK)]
    kscrs = [dict(sq=ar.alloc([128, 256], F32, "ksq%d" % i), t1=ar.alloc([128, 256], F32, "kt1%d" % i),
                  AB=ar.alloc([128, 128], F32, "kAB%d" % i), st=ar.alloc([128, 96], F32, "kst%d" % i))
             for i in range(NK)]
    krs = [ar.alloc([128, 256], BF16, "kr%d" % i) for i in range(3)]
    qraws = [ar.alloc([128, 1024], F32, "qraws%d" % i) for i in range(1)]
    qscrs = [dict(sq=ar.alloc([128, 1024], F32, "qssq%d" % i), t1=ar.alloc([128, 1024], F32, "qst1%d" % i),
                  AB=ar.alloc([128, 128], F32, "qsAB%d" % i), st=ar.alloc([128, 96], F32, "qsst%d" % i))
             for i in range(1)]
    qrs = [ar.alloc([128, 1024], BF16, "qrs%d" % i) for i in range(1)]
    qTst = [ar.alloc([128, 8, 128], BF16, "qTst%d" % i) for i in range(2)]
    pQ = T(pview(6, 2), "pQ")

    chunksA = [("P", c) for c in range(SEQ_P // 128)] + [("S", c) for c in range(SEQ_S // 128)]
    if stop_after == "A0":
        chunksA = chunksA[:4]

    def stagesA(ci):
        seg, c = chunksA[ci]
        xsrc = xp_full if seg == "P" else xs_in
        gc = c if seg == "P" else SEQ_P // 128 + c
        xt, rt, xn = xin[ci % 4], ropet[ci % 12], xnTa[ci % 3]
        kraw, kscr, kr = kraws[ci % NK], kscrs[ci % NK], krs[ci % 3]
        kvb = PB[1 + ci % 2]
        tbk = 3 + ci % 2

        def load():
            DMA("sp", xt.v, xsrc[c * 128:(c + 1) * 128, :], [], [xt])
            DMA("sp", rt.v, rope_seq[c * 128:(c + 1) * 128, :], [], [rt])
        n_ops = norm_T_ops(nctx, xt, 128, 0, xn.v, xn)

        def kvmm():
            def mmkv(e):
                for k in range(8):
                    i = e.matmul(kvb.v, lhsT=xn.v[:, k, :], rhs=wkv.v[:, k, :], start=(k == 0), stop=(k == 7))
                return i
            PE(mmkv, [xn, wkv], [kvb])

        def kvcopy():
            ACT(kraw.v, kvb.v, AF.Copy, [kvb], [kraw])
        q_ops = qk_post_ops(kraw, 4, rt, 128, kscr, kr)

        def vcopy_and_q12():
            CP("pool", Vx.v[:, gc, :, 0:64], kraw.v[:, 256:512].rearrange("p (g d) -> p g d", g=4), [kraw], [Vx])
            q_ops[0]()
            q_ops[1]()

        def ktr():
            tb = PB[tbk]
            pt = bf16_bank(tbk)

            def trk(e):
                for j in range(2):
                    i = e.transpose(out=pt[:, j, :], in_=kr.v[:, j * 128:(j + 1) * 128], identity=ident.v)
                return i
            PE(trk, [kr, ident], [tb])

        def kcopy():
            pt = bf16_bank(tbk)
            ACT(kT.v[:, :, gc * 128:(gc + 1) * 128], pt[:, 0:2, :], AF.Copy, [PB[tbk]], [kT])
        stages = [load, n_ops[0], n_ops[1], n_ops[2], n_ops[3], n_ops[4], kvmm, kvcopy, vcopy_and_q12,
                  q_ops[2], q_ops[3], q_ops[4], ktr, kcopy]
        if seg == "S":
            qraw_, qscr_, qr_, qTt = qraws[0], qscrs[0], qrs[0], qTst[c % 2]

            def qmm():
                def mmq(e):
                    for n2 in range(2):
                        for k in range(8):
                            i = e.matmul(pQ.v[:, n2, :], lhsT=xn.v[:, k, :], rhs=wq.v[:, k, n2 * 512:(n2 + 1) * 512],
                                         start=(k == 0), stop=(k == 7))
                    return i
                PE(mmq, [xn, wq], [pQ])

            def qcopy():
                ACT(qraw_.v.rearrange("p (a b) -> p a b", a=2), pQ.v, AF.Copy, [pQ], [qraw_])
            qq = qk_post_ops(qraw_, 16, rt, 0, qscr_, qr_)

            def qtr():
                pt = bf16_bank(6)

                def trq(e):
                    for j in range(8):
                        i = e.transpose(out=pt[:, j, :], in_=qr_.v[:, j * 128:(j + 1) * 128], identity=ident.v)
                    return i
                PE(trq, [qr_, ident], [pQ])

            def qTcopy():
                ACT(qTt.v, bf16_bank(6), AF.Copy, [pQ], [qTt])

            def qstore():
                DMA("sp", QS[c].rearrange("p (a b) -> p a b", a=8), qTt.v, [qTt], [])
            extra = [qmm, qcopy, lambda: (qq[0](), qq[1]()), qq[2], qq[3], qq[4], qtr, qTcopy, qstore]
            for k_, f_ in enumerate(extra):
                idx = 6 + k_
                if idx < len(stages):
                    stages[idx] = (lambda a_=stages[idx], b_=f_: (a_(), b_()))
                else:
                    stages.append(f_)
        return stages

    startsA = []
    t_ = 0
    for (seg_, c_) in chunksA:
        startsA.append(t_)
        t_ += 1 if seg_ == "P" else 4
    pipeline(len(chunksA), stagesA, startsA)
    R.barrier()
    if stop_after in ("A", "A0"):
        dbg_k = nc.dram_tensor("dbg_k", [128, 2 * NKT], BF16, kind="ExternalOutput").ap()
        dbg_v = nc.dram_tensor("dbg_v", [128, (NKT // 128) * 4 * 66], BF16, kind="ExternalOutput").ap()
        DMA("sp", dbg_k.rearrange("p (a b) -> p a b", a=2), kT.v, [kT], [], is_out=True)
        DMA("sp", dbg_v.rearrange("p (a b c) -> p a b c", a=NKT // 128, b=4), Vx.v, [Vx], [], is_out=True)
        with ExitStack() as es_:
            st = R.emit(es_)
        return nc, st

    ar.reset(abmark)
    wo = ar.alloc([128, 16, 1024], BF16, "wo")
    DMA("pool", wo.v[0:64], w_o.rearrange("(s d) f -> d s f", d=64), [], [wo])
    stage_items = []
    if STAGE_W:
        def _st(dst, src):
            return lambda: DMA("pool", dst.rearrange("(k p) f -> p k f", p=128), src.rearrange("(k p) f -> p k f", p=128),
                               [], [])
        for l in range(2):
            for hh in range(2):
                stage_items.append(_st(WUPB[l][hh * 512:(hh + 1) * 512, :], w_up[l, hh * 512:(hh + 1) * 512, :]))
            stage_items.append(_st(WDNB[l], w_dn[l]))
        for hh in range(2):
            stage_items.append(_st(WINB[hh * 512:(hh + 1) * 512, :], w_in[hh * 512:(hh + 1) * 512, :]))
        stage_items.append(_st(WOUTB, w_out))
    from collections import deque
    nctx = NormCtx(2, tbanks=(6,))
    xin = [ar.alloc([128, D], F32, "bxin%d" % i) for i in range(2)]
    xres = [ar.alloc([128, D], F32, "bxres%d" % i) for i in range(2)]
    ropet = [ar.alloc([128, 128], F32, "bropet%d" % i) for i in range(2)]
    xnTb = [ar.alloc([128, 8, 128], BF16, "xnTb%d" % i) for i in range(2)]
    qraw = ar.alloc([128, 1024], F32, "qraw")
    qscr = dict(sq=ar.alloc([128, 1024], F32, "qsq"), t1=ar.alloc([128, 1024], F32, "qt1"),
                AB=ar.alloc([128, 128], F32, "qAB"), st=ar.alloc([128, 96], F32, "qst"))
    qr = ar.alloc([128, 1024], BF16, "qr")
    qTs = [ar.alloc([128, 8, 256], BF16, "qT%d" % i) for i in range(2)]
    PT = [ar.alloc([128, 1024], BF16, "PT%d" % i) for i in range(3)]
    osb = ar.alloc([128, 1024], F32, "osb")
    rrow = ar.alloc([128, 1024], F32, "rrow")
    rhl = ar.alloc([128, 2048], BF16, "rhl")
    OTn = ar.alloc([128, 2, 8, 256], BF16, "OTn")
    pS = [T(pview(0, 2), "pS0"), T(pview(2, 2), "pS1")]
    pO = T(pview(4, 2), "pO")
    B6, B7 = PB[6], PB[7]
    PB4_saved = PB[4]

    tilesB = [("P", t) for t in range(WIN // 256)] + [("S", t) for t in range(SEQ_S // 256)]
    if stop_after == "B0":
        tilesB = [("P", 1), ("P", 2), ("S", 0), ("S", 1)]
    dq = deque()

    def pump(k=1):
        for _ in range(k):
            if dq:
                dq.popleft()()

    def flush():
        while dq:
            dq.popleft()()

    def tile_src(seg):
        return (xp_win if seg == "P" else xs_in), (rope_win if seg == "P" else rope_seq)

    def prologue_items(ti):
        seg, t = tilesB[ti]
        xsrc, rsrc = tile_src(seg)
        qT = qTs[ti % 2]
        items = []
        if seg == "S":
            def ldq():
                for h in range(2):
                    DMA("sp", qT.v[:, :, h * 128:(h + 1) * 128], QS[2 * t + h].rearrange("p (a b) -> p a b", a=8),
                        [], [qT])
            return [ldq]
        for h in range(2):
            r0 = t * 256 + h * 128
            xt, rt, xn = xin[h], ropet[h], xnTb[h]

            def p1(xt=xt, rt=rt, r0=r0):
                DMA("sp", xt.v, xsrc[r0:r0 + 128, :], [], [xt])
                DMA("sp", rt.v, rsrc[r0:r0 + 128, :], [], [rt])
            items.append(p1)
            def nrm(xt=xt, xn=xn):
                o = norm_T_ops(nctx, xt, 128, 0, xn.v, xn)
                dq_bg.extendleft(reversed([o[0], o[1], o[2], lambda: (o[3](), o[4]())]))
            items.append(nrm)

            def mm(n2, xn=xn):
                def mmq(e):
                    for k in range(8):
                        i = e.matmul(B7.v, lhsT=xn.v[:, k, :], rhs=wq.v[:, k, n2 * 512:(n2 + 1) * 512],
                                     start=(k == 0), stop=(k == 7))
                    return i
                PE(mmq, [xn, wq], [B7])

            def cp(n2):
                CP("dve", qraw.v[:, n2 * 512:(n2 + 1) * 512], B7.v, [B7], [qraw])
            items += [lambda mm=mm, cp=cp: (mm(0), cp(0)), lambda mm=mm, cp=cp: (mm(1), cp(1))]
            qo = qk_post_ops(qraw, 16, rt, 0, qscr, qr)
            items += [lambda qo=qo: (qo[0](), qo[1]()), qo[2], qo[3], qo[4]]

            def tr():
                pt = bf16_bank(6)

                def trq(e):
                    for j in range(8):
                        i = e.transpose(out=pt[:, j, :], in_=qr.v[:, j * 128:(j + 1) * 128], identity=ident.v)
                    return i
                PE(trq, [qr, ident], [B6])

            def cpq(h=h):
                CP("dve", qT.v[:, :, h * 128:(h + 1) * 128], bf16_bank(6), [B6], [qT])
            items += [lambda tr=tr, cpq=cpq: (tr(), cpq())]
        return items

    def unit_epilogue_items(pi0):
        def g1():
            ACT(rrow.v[64:65, :], osb.v[64:65, :], AF.Ln, [osb], [rrow])
            ACT(rrow.v[64:65, :], rrow.v[64:65, :], AF.Exp, [rrow], [rrow], scale=-1.0)
            CP("dve", rhl.v[64:65, 0:1024], rrow.v[64:65, :], [rrow], [rhl])
            TT("dve", rhl.v[64:65, 1024:2048], rrow.v[64:65, :], rhl.v[64:65, 0:1024], ALU.subtract, [rrow, rhl], [rhl])

        def g2():
            for par, bk in ((0, B6), (1, B7)):
                def bc(e, par=par, bk=bk):
                    e.matmul(bk.v[0:64, :], lhsT=onesb.v[64:65, :], rhs=rhl.v[64:65, par * 512:(par + 1) * 512],
                             start=True, stop=False)
                    return e.matmul(bk.v[0:64, :], lhsT=onesb.v[64:65, :],
                                    rhs=rhl.v[64:65, 1024 + par * 512:1024 + (par + 1) * 512], start=False, stop=True)
                PE(bc, [rhl, onesb], [bk])

        def g3():
            for par, bk in ((0, B6), (1, B7)):
                TT("dve", OTn.v[0:64, par, pi0:pi0 + 2, :],
                   osb.v[0:64, par * 512:(par + 1) * 512].rearrange("p (a t) -> p a t", a=2),
                   bk.v[0:64, :].rearrange("p (a t) -> p a t", a=2), ALU.mult, [osb, bk], [OTn])
        return [g1, lambda: (g2(), g3())]

    def tile_epilogue_items(ti):
        seg, t = tilesB[ti]
        base, _ = SEG[seg]
        xsrc, _r = tile_src(seg)
        items = []
        for h in range(2):
            r0 = t * 256 + h * 128
            xt = xres[h]

            def w1(xt=xt, r0=r0):
                DMA("sp", xt.v, xsrc[r0:r0 + 128, :], [], [xt])

            def wmm(n2, h=h):
                bk = B6 if n2 == 0 else B7

                def mmo(e):
                    for s_ in range(16):
                        i = e.matmul(bk.v, lhsT=OTn.v[0:64, s_ % 2, s_ // 2, h * 128:(h + 1) * 128],
                                     rhs=wo.v[0:64, s_, n2 * 512:(n2 + 1) * 512], start=(s_ == 0), stop=(s_ == 15))
                    return i
                PE(mmo, [OTn, wo], [bk])

            def wadd(n2, xt=xt):
                bk = B6 if n2 == 0 else B7
                TT("dve", xt.v[:, n2 * 512:(n2 + 1) * 512], bk.v, xt.v[:, n2 * 512:(n2 + 1) * 512], ALU.add,
                   [bk, xt], [xt])

            def w3(xt=xt, r0=r0):
                DMA("sp", X1[base + r0:base + r0 + 128, :], xt.v, [xt], [])
            items += [w1, lambda wmm=wmm, wadd=wadd: (wmm(0), wadd(0)), lambda wmm=wmm, wadd=wadd: (wmm(1), wadd(1)), w3]
        return items

    dq_urgent = deque()
    dq_bg = deque()

    def pump(k=1):
        for _ in range(k):
            if dq_urgent:
                dq_urgent.popleft()()
            elif dq_bg:
                dq_bg.popleft()()

    def flush_urgent():
        while dq_urgent:
            dq_urgent.popleft()()

    def flush_all():
        while dq_urgent or dq_bg:
            pump()

    sidx = 0
    dq_bg.extend(prologue_items(0))
    flush_all()
    for ti, (seg, t) in enumerate(tilesB):
        nk = (SEQ_P if seg == "P" else SEQ_S) // 128
        kbase = 0 if seg == "P" else SEQ_P
        qT = qTs[ti % 2]
        if ti + 1 < len(tilesB):
            dq_bg.extend(prologue_items(ti + 1))
        if stage_items:
            dq_bg.append(stage_items.pop(0))
        units = [(gp, j0) for gp in range(2) for j0 in (0, 2)]
        for (gp, j0) in units:
            pi0 = 4 * gp + j0

            def S_mm(e, kc, slot, gp=gp, pi0=pi0, kbase=kbase, qT=qT):
                k0 = kbase + kc * 128
                e.matmul(pS[slot].v[:, 0, :], lhsT=kT.v[0:64, gp, k0:k0 + 128], rhs=qT.v[0:64, pi0:pi0 + 2, :],
                         start=True, stop=True)
                return e.matmul(pS[slot].v[:, 1, :], lhsT=kT.v[64:128, gp, k0:k0 + 128],
                                rhs=qT.v[64:128, pi0:pi0 + 2, :], start=True, stop=True)

            def PV_mm(e, kc, pt, gp=gp, nk=nk, kbase=kbase):
                gck = (kbase // 128) + kc
                e.matmul(pO.v[0:65, 0, :], lhsT=Vx.v[:, gck, 2 * gp, 0:65], rhs=pt.v[:, 0:512],
                         start=(kc == 0), stop=(kc == nk - 1))
                return e.matmul(pO.v[0:65, 1, :], lhsT=Vx.v[:, gck, 2 * gp + 1, 0:65], rhs=pt.v[:, 512:1024],
                                start=(kc == 0), stop=(kc == nk - 1))

            pend = None
            for kc in range(nk):
                slot = sidx % 2
                ptile = PT[sidx % 3]
                sidx += 1
                PE(lambda e, f=S_mm, kc=kc, slot=slot: f(e, kc, slot), [kT, qT], [pS[slot]])
                ACT(ptile.v.rearrange("p (a b) -> p a b", a=2), pS[slot].v, AF.Exp, [pS[slot]], [ptile],
                    scale=1.0 / math.sqrt(HD))
                if pend is not None:
                    PE(lambda e, f=PV_mm, a=pend: f(e, a[0], a[1]), [Vx, pend[1]], [pO])
                pend = (kc, ptile)
                if kc >= 1 and (nk <= 16 or kc % 2 == 0):
                    pump()
            PE(lambda e, f=PV_mm, a=pend: f(e, a[0], a[1]), [Vx, pend[1]], [pO])
            flush_urgent()
            CP("dve", osb.v[0:65].rearrange("p (a b) -> p a b", a=2), pO.v[0:65], [pO], [osb])
            dq_urgent.extend(unit_epilogue_items(pi0))
        dq_urgent.extend(tile_epilogue_items(ti))
        flush_all() if ti + 1 == len(tilesB) else None
        while dq_bg:
            dq_bg.popleft()()
    while stage_items:
        stage_items.pop(0)()
    R.barrier()
    PB[4] = PB4_saved
    if stop_after in ("B", "B0"):
        dbg = nc.dram_tensor("dbg_x1", [NROW, D], F32, kind="ExternalOutput").ap()
        DMA("sp", dbg, X1, [], [], is_out=True)
        with ExitStack() as es_:
            st = R.emit(es_)
        return nc, st

    def ffn_phase(layer, Xin, jobs):
        ar.reset(pmark)
        wup = ar.alloc([128, 8, 2 * FF], BF16, "wup")
        wdn = ar.alloc([128, 22, D], BF16, "wdn")
        cw = ar.alloc([128, 44, 3], F32, "cw")
        cb = ar.alloc([128, 44], F32, "cb")
        wupb = [Buf("wup%d" % k) for k in range(8)]
        wq_eng = "sp" if STAGE_W else "pool"
        wsrc_up = WUPB[layer] if STAGE_W else w_up[layer]
        wsrc_dn = WDNB[layer] if STAGE_W else w_dn[layer]
        for k in range(8):
            R.op(wq_eng, lambda e, k=k: e.dma_start(out=wup.v[:, k, :], in_=wsrc_up[k * 128:(k + 1) * 128, :]),
                 [], [wupb[k]], dma=True)
        for hh in range(2):
            R.op(wq_eng, lambda e, hh=hh: e.dma_start(
                out=wdn.v[:, hh * 11:(hh + 1) * 11, :],
                in_=wsrc_dn[hh * 11 * 128:(hh + 1) * 11 * 128, :].rearrange("(i p) f -> p i f", p=128)),
                [], [wdn.b], dma=True)
        DMA("sp", cw.v, convw_in[layer].rearrange("p (a b) -> p a b", a=44), [], [cw])
        DMA("sp", cb.v, convb_in[layer], [], [cb])
        nctx = NormCtx(2)
        nctx.scale_eng = "act"
        nctx.tbanks = [0]
        xnTs = [ar.alloc([128, 8, 512], BF16, "fxnT%d" % i) for i in range(2)]
        gT = ar.alloc([128, 22, 512], BF16, "fgT")
        tg = [ar.alloc([128, 512], F32, "ftg%d" % i) for i in range(2)]
        tu = [ar.alloc([128, 512], F32, "ftu%d" % i) for i in range(2)]
        sg = [ar.alloc([128, 512], F32, "fsg%d" % i) for i in range(2)]
        gidx = 1 if layer == 0 else 3
        cnt = dict(xi=0, pidx=0)
        tiles = []
        for (seg, s, e_, outd, orow) in jobs:
            for (ts_, n) in split(e_ - s, 510):
                tiles.append(dict(seg=seg, t0=s + ts_, n=n, outd=outd, orow=orow))

        xb = [ar.alloc([128, D], F32, "fxb%d" % i) for i in range(2)]
        xd = [ar.alloc([128, D], F32, "fxd%d" % i) for i in range(2)]

        def build_items(ti):
            tl = tiles[ti]
            seg, t0, n = tl["seg"], tl["t0"], tl["n"]
            base, slen = SEG[seg]
            xnT = xnTs[ti % 2]
            ncol = n + 2
            c_lo = t0 - 1
            lo = max(c_lo, 0)
            hi = min(c_lo + ncol, slen)

            def edges():
                if c_lo < 0:
                    MEMSET("pool", xnT.v[:, :, 0:1], 0.0, [xnT])
                if c_lo + ncol > slen:
                    MEMSET("pool", xnT.v[:, :, ncol - 1:ncol], 0.0, [xnT])
            items = []
            for (ss, sn) in split(hi - lo, 128):
                r0 = lo + ss
                col = r0 - c_lo
                xt = xb[cnt["xb"] % 2]
                cnt["xb"] += 1
                ops = norm_T_ops(nctx, xt, sn, gidx, xnT.v[:, :, col:col + sn], xnT)
                items.append(dict(load=(lambda xt=xt, r0=r0, sn=sn: load_rows(xt, Xin, base + r0, sn)), ops=ops))

            def masks():
                if seg == "P":
                    for mi, mcol in ((0, HALO - 1), (1, WIN - HALO)):
                        if c_lo <= mcol < c_lo + ncol:
                            cc = mcol - c_lo
                            TS("dve", xnT.v[:, :, cc:cc + 1], xnT.v[:, :, cc:cc + 1], maskt.v[:, mi:mi + 1], None,
                               ALU.mult, None, [xnT, maskt], [xnT])
            return edges, items, masks

        def up_loop(ti):
            tl = tiles[ti]
            n = tl["n"]
            ncol = n + 2
            xnT = xnTs[ti % 2]
            prev_gate = None
            for i in range(22):
                pidx = cnt["pidx"]
                cnt["pidx"] += 1
                pg = PB[1 + 2 * (pidx % 2)]
                pu = PB[2 + 2 * (pidx % 2)]
                sl = pidx % 2

                def mmup(e, i=i, pg=pg, pu=pu, ncol=ncol, xnT=xnT):
                    for (pp, f0) in ((pg, i * 128), (pu, FF + i * 128)):
                        for k in range(8):
                            ins = e.matmul(pp.v[:, 0:ncol], lhsT=wup.v[:, k, f0:f0 + 128], rhs=xnT.v[:, k, 0:ncol],
                                           start=(k == 0), stop=(k == 7))
                    return ins
                PE(mmup, [xnT] + wupb, [pg, pu])
                for (pp, tt_, ci) in ((pg, tg[sl], i), (pu, tu[sl], 22 + i)):
                    ACT(tt_.v[:, 0:n], pp.v[:, 1:n + 1], AF.Identity, [pp, cw, cb], [tt_],
                        scale=cw.v[:, ci, 1:2], bias=cb.v[:, ci:ci + 1])
                for (pp, tt_, ci) in ((pg, tg[sl], i), (pu, tu[sl], 22 + i)):
                    STT("dve", tt_.v[:, 0:n], pp.v[:, 0:n], cw.v[:, ci, 0:1], tt_.v[:, 0:n], ALU.mult, ALU.add,
                        [pp, cw, tt_], [tt_])
                    STT("dve", tt_.v[:, 0:n], pp.v[:, 2:n + 2], cw.v[:, ci, 2:3], tt_.v[:, 0:n], ALU.mult, ALU.add,
                        [pp, cw, tt_], [tt_])

                def gate(i=i, sl=sl):
                    ACT(sg[sl].v[:, 0:n], tg[sl].v[:, 0:n], AF.Silu, [tg[sl]], [sg[sl]])
                    TT("pool", gT.v[:, i, 0:n], sg[sl].v[:, 0:n], tu[sl].v[:, 0:n], ALU.mult, [sg[sl], tu[sl]], [gT])
                if prev_gate is not None:
                    prev_gate()
                prev_gate = gate
            prev_gate()

        def down_items(ti):
            tl = tiles[ti]
            seg, t0, n, outd, orow = tl["seg"], tl["t0"], tl["n"], tl["outd"], tl["orow"]
            base, slen = SEG[seg]
            items = []
            for (ss, sn) in split(n, 128):
                r0 = t0 + ss
                xt = xd[cnt["xd"] % 2]
                cnt["xd"] += 1
                pdb = [PB[5], PB[6]]

                def mm(ss=ss, sn=sn, pdb=pdb):
                    def mmdn(e):
                        for n2 in range(2):
                            for i in range(22):
                                ins = e.matmul(psum[0:sn, 5 + n2, :], lhsT=gT.v[:, i, ss:ss + sn],
                                               rhs=wdn.v[:, i, n2 * 512:(n2 + 1) * 512], start=(i == 0), stop=(i == 21))
                        return ins
                    PE(mmdn, [gT, wdn], pdb)

                def fin(xt=xt, sn=sn, r0=r0, pdb=pdb):
                    TT("dve", xt.v[0:sn].rearrange("p (a b) -> p a b", a=2), psum[0:sn, 5:7, :],
                       xt.v[0:sn].rearrange("p (a b) -> p a b", a=2), ALU.add, pdb + [xt], [xt])
                    store_rows(outd, r0 + orow, xt, sn, is_out=True)
                items.append(dict(load=(lambda xt=xt, r0=r0, sn=sn: load_rows(xt, Xin, base + r0, sn)), mm=mm, fin=fin))
            return items

        cnt["xb"] = 0
        cnt["xd"] = 0
        pre, bl, post = build_items(0)
        pre()
        interleave([], bl)
        post()
        for ti in range(len(tiles)):
            up_loop(ti)
            dl = down_items(ti)
            if ti + 1 < len(tiles):
                pre, bl, post = build_items(ti + 1)
                pre()
                interleave(dl, bl)
                post()
            else:
                interleave(dl, [])
        R.barrier()

    ffn_phase(0, X1, [("P", 128, WIN - 128, X2, 0), ("S", 0, SEQ_S, X2, WIN)])
    if stop_after == "C":
        dbg = nc.dram_tensor("dbg_x2", [NROW, D], F32, kind="ExternalOutput").ap()
        DMA("sp", dbg, X2, [], [], is_out=True)
        with ExitStack() as es_:
            st = R.emit(es_)
        return nc, st

    ar.reset(pmark)
    win = ar.alloc([128, 8, 4096], BF16, "win")
    wout = ar.alloc([128, 16, D], BF16, "wout")
    wsT = ar.alloc([128, 8, 128], BF16, "wsT")
    vgain = ar.alloc([128, 2048], F32, "vgain")
    bsT = ar.alloc([128, 16, 128], F32, "bsT")
    winb = [Buf("win%d" % k) for k in range(8)]
    wq_eng = "sp" if STAGE_W else "pool"
    for k in range(8):
        R.op(wq_eng, lambda e, k=k: e.dma_start(out=win.v[:, k, :], in_=(WINB if STAGE_W else w_in)[k * 128:(k + 1) * 128, :]),
             [], [winb[k]], dma=True)
    DMA(wq_eng, wout.v, (WOUTB if STAGE_W else w_out).rearrange("(i p) f -> p i f", p=128), [], [wout])
    DMA("pool", wsT.v, ws_T.rearrange("p (g q) -> p g q", g=8), [], [wsT])
    DMA("sp", vgain.v, vgain_in.partition_broadcast(128), [], [vgain])
    DMA("sp", bsT.v, bsT_in.partition_broadcast(128).rearrange("p (a b) -> p a b", a=16), [], [bsT])
    nctx = NormCtx(2)
    nctx.tbanks = [0]
    nctx.rsqrt_eng = "dve"
    xnTs = [ar.alloc([128, 8, 512], BF16, "dxnT%d" % i) for i in range(2)]
    uT = ar.alloc([128, 16, 512], BF16, "duT")
    yT = ar.alloc([128, 16, 512], BF16, "dyT")
    vgs = [ar.alloc([128, 2048], F32, "dvg%d" % i) for i in range(2)]
    vhats = [ar.alloc([128, 2048], BF16, "dvhat%d" % i) for i in range(2)]
    vsts = [ar.alloc([128, 4], F32, "dvst%d" % i) for i in range(2)]
    tsb = [ar.alloc([128, 512], F32, "dts%d" % i) for i in range(1)]
    cntD = dict(xi=0, ui=0, vi=0, si=0, ci=0)
    tilesD = []
    for (seg, c0, c1) in [("P", 1, 19), ("S", 0, 16)]:
        for (ts_, ncn) in split(c1 - c0, 4):
            tilesD.append(dict(seg=seg, cs=c0 + ts_, ncn=ncn))

    xbD = [ar.alloc([128, D], F32, "dxb%d" % i) for i in range(1)]
    xdD = [ar.alloc([128, D], F32, "dxd%d" % i) for i in range(2)]
    cntD["xb"] = 0
    cntD["xd"] = 0

    def D_build_items(ti):
        tl = tilesD[ti]
        base, slen = SEG[tl["seg"]]
        xnT = xnTs[ti % 2]
        items = []
        for j in range(tl["ncn"]):
            r0 = (tl["cs"] + j) * 128
            xt = xbD[0]
            cntD["xb"] += 1
            ops = norm_T_ops(nctx, xt, 128, 2, xnT.v[:, :, j * 128:(j + 1) * 128], xnT)
            items.append(dict(load=(lambda xt=xt, r0=r0: DMA("sp", xt.v, X2[base + r0:base + r0 + 128, :], [], [xt])),
                              ops=ops))
        return items

    def D_U(ti):
        tl = tilesD[ti]
        ncol = tl["ncn"] * 128
        xnT = xnTs[ti % 2]
        for fc in range(16):
            pb_ = PB[1 + cntD["ui"] % 2]
            cntD["ui"] += 1

            def mmu(e, fc=fc, pb_=pb_):
                for k in range(8):
                    ins = e.matmul(pb_.v[:, 0:ncol], lhsT=win.v[:, k, fc * 128:(fc + 1) * 128], rhs=xnT.v[:, k, 0:ncol],
                                   start=(k == 0), stop=(k == 7))
                return ins
            PE(mmu, [xnT] + winb, [pb_])
            ACT(uT.v[:, fc, 0:ncol], pb_.v[:, 0:ncol], AF.Gelu_apprx_tanh, [pb_], [uT])

    def D_V(ti, j, ci):
        xnT = xnTs[ti % 2]
        vg = vgs[ci % 2]
        for qd in range(4):
            pb_ = PB[3 + cntD["vi"] % 2]
            cntD["vi"] += 1

            def mmv(e, qd=qd, pb_=pb_):
                for k in range(8):
                    ins = e.matmul(pb_.v, lhsT=xnT.v[:, k, j * 128:(j + 1) * 128],
                                   rhs=win.v[:, k, 2048 + qd * 512:2048 + (qd + 1) * 512],
                                   start=(k == 0), stop=(k == 7))
                return ins
            PE(mmv, [xnT] + winb, [pb_])
            ACT(vg.v[:, qd * 512:(qd + 1) * 512], pb_.v, AF.Gelu_apprx_tanh, [pb_], [vg])

    def D_N(ci):
        vhat, vst = vhats[ci % 2], vsts[ci % 2]
        vg = vgs[ci % 2]
        ACT(vhat.v, vg.v, AF.Square, [vg], [vhat, vst], accum_out=vst.v[:, 0:1])
        rsqrt_dve(vst, 128, 1.0 / 2048)
        STT("dve", vhat.v, vg.v, vst.v[:, 2:3], vgain.v, ALU.mult, ALU.mult, [vg, vst, vgain], [vhat])

    def D_S(ci, j):
        vhat = vhats[ci % 2]
        for q4 in range(4):
            pb_ = PB[5 + cntD["si"] % 2]
            tt_ = tsb[0]
            cntD["si"] += 1

            def mms(e, q4=q4, pb_=pb_):
                for f in range(4):
                    fc = q4 * 4 + f
                    ins = e.matmul(pb_.v[:, f * 128:(f + 1) * 128], lhsT=vhat.v[:, fc * 128:(fc + 1) * 128],
                                   rhs=wsT.v[:, fc // 2, :], start=True, stop=True)
                return ins
            PE(mms, [vhat, wsT], [pb_])
            TT("dve", tt_.v.rearrange("p (a b) -> p a b", a=4), pb_.v.rearrange("p (a b) -> p a b", a=4),
               bsT.v[:, q4 * 4:(q4 + 1) * 4, :], ALU.add, [pb_, bsT], [tt_])
            TT("pool", yT.v[:, q4 * 4:(q4 + 1) * 4, j * 128:(j + 1) * 128],
               tt_.v.rearrange("p (a b) -> p a b", a=4), uT.v[:, q4 * 4:(q4 + 1) * 4, j * 128:(j + 1) * 128],
               ALU.mult, [tt_, uT], [yT])

    def D_O_items(ti):
        tl = tilesD[ti]
        base, slen = SEG[tl["seg"]]
        items = []
        for j in range(tl["ncn"]):
            r0 = (tl["cs"] + j) * 128
            xt = xdD[cntD["xd"] % 2]
            cntD["xd"] += 1
            pdb = [PB[1], PB[2]]

            def mm(j=j, pdb=pdb):
                def mmout(e):
                    for n2 in range(2):
                        for fc in range(16):
                            ins = e.matmul(psum[:, 1 + n2, :], lhsT=yT.v[:, fc, j * 128:(j + 1) * 128],
                                           rhs=wout.v[:, fc, n2 * 512:(n2 + 1) * 512], start=(fc == 0), stop=(fc == 15))
                    return ins
                PE(mmout, [yT, wout], pdb)

            def fin(xt=xt, r0=r0, pdb=pdb):
                TT("dve", xt.v.rearrange("p (a b) -> p a b", a=2), psum[:, 1:3, :],
                   xt.v.rearrange("p (a b) -> p a b", a=2), ALU.add, pdb + [xt], [xt])
                DMA("sp", X3[base + r0:base + r0 + 128, :], xt.v, [xt], [])
            items.append(dict(load=(lambda xt=xt, r0=r0: DMA("sp", xt.v, X2[base + r0:base + r0 + 128, :], [], [xt])),
                              mm=mm, fin=fin))
        return items

    interleave([], D_build_items(0))
    for ti in range(len(tilesD)):
        ncn = tilesD[ti]["ncn"]
        D_U(ti)
        pendS = None
        for j in range(ncn):
            ci = cntD["ci"]
            cntD["ci"] += 1
            D_V(ti, j, ci)
            D_N(ci)
            if pendS is not None:
                D_S(*pendS)
            pendS = (ci, j)
        D_S(*pendS)
        interleave(D_O_items(ti), D_build_items(ti + 1) if ti + 1 < len(tilesD) else [])
    R.barrier()
    if stop_after == "D":
        dbg = nc.dram_tensor("dbg_x3", [NROW, D], F32, kind="ExternalOutput").ap()
        DMA("sp", dbg, X3, [], [], is_out=True)
        with ExitStack() as es_:
            st = R.emit(es_)
        return nc, st

    ffn_phase(1, X3, [("P", HALO, WIN - HALO, y_p, -HALO), ("S", 0, SEQ_S, y_s, 0)])
    with ExitStack() as es_:
        st = R.emit(es_)
    return nc, st


def rope_table(pos):
    pos = np.asarray(pos, dtype=np.int64)
    row = (pos // 64).astype(np.float32)
    col = (pos % 64).astype(np.float32)
    inv = (10000.0 ** (-np.arange(0, 32, 2, dtype=np.float32) / np.float32(32))).astype(np.float32)
    ar_ = row[:, None] * inv[None, :]
    ac_ = col[:, None] * inv[None, :]
    ang = np.concatenate([ar_, ar_, ac_, ac_], axis=-1).astype(np.float32)
    cos = np.cos(ang).astype(np.float32)
    sin = np.sin(ang).astype(np.float32)
    sgn = np.concatenate([-np.ones(16), np.ones(16), -np.ones(16), np.ones(16)]).astype(np.float32)
    return np.ascontiguousarray(np.concatenate([cos, sin * sgn[None, :]], axis=-1).astype(np.float32))


def swap16(g):
    g = np.asarray(g, dtype=np.float32).reshape(2, 2, 16)
    return np.ascontiguousarray(g[:, ::-1, :]).reshape(64)


def prep_inputs(inp):
    f = lambda a: np.ascontiguousarray(np.asarray(a, dtype=np.float32))
    x_prompt, x_sample = f(inp["x_prompt"]), f(inp["x_sample"])
    wqkv = f(inp["attn_w_qkv"])[0]
    perm = np.concatenate([np.arange(head_of_slot(s) * 64, head_of_slot(s) * 64 + 64) for s in range(16)])
    wqkv_p = np.ascontiguousarray(np.concatenate([wqkv[:, perm], wqkv[:, 1024:]], axis=1))
    wo_p = np.ascontiguousarray(f(inp["attn_w_o"])[0][perm, :])
    gq, gk = f(inp["attn_q_norm"])[0], f(inp["attn_k_norm"])[0]
    gqk = np.ascontiguousarray(np.concatenate([gq, swap16(gq), gk, swap16(gk)]).astype(np.float32))
    nm, nf = f(inp["norm_mix"]), f(inp["norm_ffn"])
    gcols = np.stack([nm[0], nf[0], nm[1], nf[1]], 0).reshape(4, 8, 128).transpose(2, 0, 1).reshape(128, 32)
    ws = f(inp["sgu_w_s"])[0]
    ws_T = np.ascontiguousarray(ws.transpose(2, 0, 1)).reshape(128, 8 * 128)
    bs = f(inp["sgu_b_s"])[0]
    bsT = np.ascontiguousarray(np.repeat(bs, 2, axis=0)).reshape(2048)
    cwt = f(inp["ffn_conv_w"])
    convw = np.ascontiguousarray(cwt.reshape(2, 3, 44, 128).transpose(0, 3, 2, 1)).reshape(2, 128, 132)
    convb = np.ascontiguousarray(f(inp["ffn_conv_b"]).reshape(2, 44, 128).transpose(0, 2, 1))
    shared = dict(
        rope_seq=rope_table(np.arange(SEQ_P)), w_qkv=wqkv_p, w_o=wo_p, w_in=f(inp["sgu_w_in"])[0],
        w_out=f(inp["sgu_w_out"])[0], ws_T=ws_T, w_up=f(inp["ffn_w_up"]), w_dn=f(inp["ffn_w_down"]),
        gcols=np.ascontiguousarray(gcols.astype(np.float32)), gqk=gqk, vgain=f(inp["sgu_v_norm"]).reshape(2048),
        bsT=bsT, convw=convw, convb=convb)
    maps = []
    for c in range(8):
        b, q = c // 4, c % 4
        a0 = 2048 * q - HALO
        win = np.zeros((WIN, D), np.float32)
        lo, hi = max(a0, 0), min(a0 + WIN, SEQ_P)
        win[lo - a0:hi - a0] = x_prompt[b, lo:hi]
        pos = np.clip(np.arange(a0, a0 + WIN), 0, SEQ_P - 1)
        mask = np.zeros((128, 2), np.float32)
        mask[:, 0] = 0.0 if q == 0 else 1.0
        mask[:, 1] = 0.0 if q == 3 else 1.0
        m = dict(shared)
        m.update(xp_full=x_prompt[b], xp_win=win, xs=x_sample[c], rope_win=rope_table(pos), mask=mask)
        maps.append(m)
    return maps


_NC_CACHE = {}


def kernel(**inputs):
    maps = prep_inputs(inputs)
    if "nc" not in _NC_CACHE:
        _NC_CACHE["nc"] = build()[0]
    nc = _NC_CACHE["nc"]
    res = run_bass_kernel_spmd(nc, maps, core_ids=list(range(8)))
    yp = np.zeros((2, SEQ_P, D), np.float32)
    ys = np.zeros((8, SEQ_S, D), np.float32)
    for c in range(8):
        r = res.results[c]
        yp[c // 4, 2048 * (c % 4):2048 * (c % 4 + 1)] = np.asarray(r["y_p"], dtype=np.float32)
        ys[c] = np.asarray(r["y_s"], dtype=np.float32)
    return yp, ys
```

```python
import math
import os
CUT = int(os.environ.get('CUT', '99'))
STAGE_W = os.environ.get('STAGE_W', '1') == '1'
from contextlib import ExitStack

import numpy as np
import concourse.bass as bass
import concourse.mybir as mybir
from concourse.bass_utils import run_bass_kernel_spmd

F32 = mybir.dt.float32
BF16 = mybir.dt.bfloat16
AF = mybir.ActivationFunctionType
ALU = mybir.AluOpType
AX = mybir.AxisListType

D = 1024
NH, HD, NKV = 16, 64, 4
FF = 2816
SEQ_P, SEQ_S = 8192, 2048
WIN = 2560
HALO = 256
EPS = 1e-6
N_DMA_SEMS = {"sp": 12, "pool": 6}


class Buf:
    __slots__ = ("name", "last_w", "readers", "dreaders")

    def __init__(self, name):
        self.name = name
        self.last_w = None
        self.readers = {}
        self.dreaders = []


class Rec:
    ENGS = ("pe", "act", "dve", "pool", "sp")

    def __init__(self, nc):
        self.nc = nc
        self.ops = {e: [] for e in self.ENGS}
        self.dma_rr = {e: 0 for e in self.ENGS}
        self.dma_last = {}
        self.last_compute = {}
        self.out_dmas = []
        self.eng_cov = {}

    def op(self, eng, fn, reads=(), writes=(), dma=False, out=False):
        ops = self.ops
        idx = len(ops[eng])
        key = (eng, idx)
        deps = set()
        for b in reads:
            w = b.last_w
            if w is not None:
                if w[0] != eng or eng != "pe" or ops[w[0]][w[1]]["dma"]:
                    deps.add(w)
        for b in writes:
            w = b.last_w
            if w is not None and (w[0] != eng or eng != "pe" or ops[w[0]][w[1]]["dma"] or dma):
                deps.add(w)
            for re_, ri in b.readers.items():
                if re_ != eng or eng != "pe" or dma:
                    deps.add((re_, ri))
            for r in b.dreaders:
                deps.add(r)
        slot = None
        if dma:
            n = N_DMA_SEMS[eng]
            slot = self.dma_rr[eng] % n
            self.dma_rr[eng] += 1
            prev = self.dma_last.get((eng, slot))
            if prev is not None:
                deps.add(prev)
            self.dma_last[(eng, slot)] = key
        else:
            self.last_compute[eng] = key
        cov = self.eng_cov.setdefault(eng, {})
        dl = sorted(deps, key=lambda d: -d[1])
        via = {}
        for d in dl:
            o = ops[d[0]][d[1]]
            if not o["dma"]:
                for e2, i2 in o["cov"].items():
                    if via.get(e2, -1) < i2:
                        via[e2] = i2
        keep = set()
        for d in dl:
            o = ops[d[0]][d[1]]
            if o["dma"]:
                keep.add(d)
            elif max(cov.get(d[0], -1), via.get(d[0], -1)) < d[1]:
                keep.add(d)
        deps = keep
        for e2, i2 in via.items():
            if cov.get(e2, -1) < i2:
                cov[e2] = i2
        for d in deps:
            if not ops[d[0]][d[1]]["dma"] and cov.get(d[0], -1) < d[1]:
                cov[d[0]] = d[1]
        ops[eng].append(dict(fn=fn, deps=deps, dma=dma, slot=slot, sig=False, tok=None, cov=dict(cov)))
        for b in reads:
            if dma:
                b.dreaders.append(key)
            else:
                b.readers[eng] = idx
        for b in writes:
            b.last_w = key
            b.readers = {}
            b.dreaders = []
        if out:
            self.out_dmas.append(key)
        return key

    def barrier(self):
        allk = set(self.last_compute.values()) | set(self.dma_last.values())
        for e in self.ENGS:
            deps = set(k for k in allk if not (k[0] == e and not self.ops[k[0]][k[1]]["dma"]))
            self.ops[e].append(dict(fn=None, deps=deps, dma=False, slot=None, sig=False, tok=None))

    def emit(self, es):
        nc = self.nc
        fin = set(self.out_dmas) | set(self.dma_last.values())
        self.ops["sp"].append(dict(fn=None, deps=fin, dma=False, slot=None, sig=False, tok=None))
        for e in self.ENGS:
            for o in self.ops[e]:
                for (de, di) in o["deps"]:
                    self.ops[de][di]["sig"] = True
        csem = {e: es.enter_context(nc.semaphore("c_" + e)) for e in self.ENGS}
        dsem = {e: [es.enter_context(nc.semaphore("d_%s%d" % (e, i))) for i in range(N_DMA_SEMS[e])]
                for e in N_DMA_SEMS}
        for e in self.ENGS:
            cnt = 0
            dcnt = [0] * N_DMA_SEMS.get(e, 0)
            for o in self.ops[e]:
                if o["dma"]:
                    dcnt[o["slot"]] += 16
                    o["tok"] = (("d", e, o["slot"]), dcnt[o["slot"]])
                elif o["sig"]:
                    cnt += 1
                    o["tok"] = (("c", e), cnt)

        def semof(k):
            return csem[k[1]] if k[0] == "c" else dsem[k[1]][k[2]]

        engobj = {"pe": "tensor", "act": "scalar", "dve": "vector", "pool": "gpsimd", "sp": "sync"}
        stats = {}
        allops = self.ops
        with nc.Block() as block:
            for e in self.ENGS:
                def body(eng, ops=allops[e], e=e):
                    waited = {}
                    nw = 0
                    for o in ops:
                        need = {}
                        for (de, di) in o["deps"]:
                            k, v = allops[de][di]["tok"]
                            if waited.get(k, 0) < v and need.get(k, 0) < v:
                                need[k] = v
                        for k, v in need.items():
                            eng.wait_ge(semof(k), v)
                            waited[k] = v
                            nw += 1
                        if o["fn"] is None:
                            continue
                        ins = o["fn"](eng)
                        if o["dma"]:
                            ins.then_inc(semof(o["tok"][0]), 16)
                        elif o["sig"]:
                            ins.then_inc(semof(o["tok"][0]), 1)
                    stats[e] = (len(ops), nw)

                getattr(block, engobj[e])(body)
        return stats


class T:
    __slots__ = ("v", "b", "extra")

    def __init__(self, v, name):
        self.v = v
        self.b = Buf(name)
        self.extra = []


class Arena:
    def __init__(self, nc, nbytes):
        self.t = nc.alloc_sbuf_tensor("arena", [128, nbytes // 2], BF16)
        self.top = 0
        self.cap = nbytes
        self.n = 0

    def mark(self):
        return self.top

    def reset(self, m):
        self.top = m

    def alloc(self, shape, dtype, name=None):
        esz = 4 if dtype == F32 else 2
        ne = 1
        for s in shape[1:]:
            ne *= s
        nb = (ne * esz + 63) // 64 * 64
        off = self.top
        self.top += nb
        assert self.top <= self.cap, ("SBUF arena overflow", name, self.top, self.cap)
        v = self.t[:, off // 2: off // 2 + ne * esz // 2]
        if dtype == F32:
            v = v.bitcast(F32)
        if len(shape) > 2:
            names = "abcde"[: len(shape) - 1]
            kw = {names[i]: shape[i + 1] for i in range(len(shape) - 1)}
            v = v.rearrange("p (%s) -> p %s" % (" ".join(names), " ".join(names)), **kw)
        self.n += 1
        return T(v, name or ("t%d" % self.n))


def head_of_slot(s):
    pi, par = s // 2, s % 2
    return 8 * (pi // 4) + 4 * par + (pi % 4)


def split(n, m):
    k = (n + m - 1) // m
    base = (n + k - 1) // k
    out = []
    s = 0
    while s < n:
        sz = min(base, n - s)
        out.append((s, sz))
        s += sz
    return out


def build(stop_after=None):
    nc = bass.Bass("TRN2", target_bir_lowering=False)

    def din(name, shape):
        return nc.dram_tensor(name, list(shape), F32, kind="ExternalInput").ap()

    xp_full = din("xp_full", [SEQ_P, D])
    xp_win = din("xp_win", [WIN, D])
    xs_in = din("xs", [SEQ_S, D])
    rope_seq = din("rope_seq", [SEQ_P, 128])
    rope_win = din("rope_win", [WIN, 128])
    mask_in = din("mask", [128, 2])
    w_qkv = din("w_qkv", [D, 1536])
    w_o = din("w_o", [D, D])
    w_in = din("w_in", [D, 4096])
    w_out = din("w_out", [2048, D])
    ws_T = din("ws_T", [128, 8 * 128])
    w_up = din("w_up", [2, D, 2 * FF])
    w_dn = din("w_dn", [2, FF, D])
    gcols_in = din("gcols", [128, 32])
    gqk_in = din("gqk", [256])
    vgain_in = din("vgain", [2048])
    bsT_in = din("bsT", [2048])
    convw_in = din("convw", [2, 128, 132])
    convb_in = din("convb", [2, 128, 44])
    y_p = nc.dram_tensor("y_p", [2048, D], F32, kind="ExternalOutput").ap()
    y_s = nc.dram_tensor("y_s", [2048, D], F32, kind="ExternalOutput").ap()
    NROW = WIN + SEQ_S
    X1 = nc.dram_tensor("X1", [NROW, D], F32).ap()
    X2 = nc.dram_tensor("X2", [NROW, D], F32).ap()
    X3 = nc.dram_tensor("X3", [NROW, D], F32).ap()
    SEG = {"P": (0, WIN), "S": (WIN, SEQ_S)}
    WUPB = [nc.dram_tensor("WUPB%d" % l, [D, 2 * FF], BF16).ap() for l in range(2)]
    WDNB = [nc.dram_tensor("WDNB%d" % l, [FF, D], BF16).ap() for l in range(2)]
    WINB = nc.dram_tensor("WINB", [D, 4096], BF16).ap()
    WOUTB = nc.dram_tensor("WOUTB", [2048, D], BF16).ap()
    QS = nc.dram_tensor("QS", [SEQ_S // 128, 128, 1024], BF16).ap()

    R = Rec(nc)
    ar = Arena(nc, 211968)
    psum = nc.alloc_psum_tensor("psum", [128, 8, 512], F32)
    PB = [T(psum[:, i, :], "bank%d" % i) for i in range(8)]

    def pview(i, n=1):
        return psum[:, i:i + n, :]

    def bf16_bank(i):
        return psum[:, i, :].bitcast(BF16).rearrange("p (k t) -> p k t", k=8)

    def bufs(ts):
        out = []
        for t in ts:
            if isinstance(t, T):
                out.append(t.b)
                out.extend(t.extra)
            else:
                out.append(t)
        return out

    def load_rows(xt, src, r0, n):
        n16 = (n // 16) * 16
        if n16 == n or n16 == 0:
            R.op("sp", lambda e: e.dma_start(out=xt.v[0:n], in_=src[r0:r0 + n, :]), [], bufs([xt]), dma=True)
            return
        if not xt.extra:
            xt.extra.append(Buf("x_tail"))
        R.op("sp", lambda e: e.dma_start(out=xt.v[0:n16], in_=src[r0:r0 + n16, :]), [], [xt.b], dma=True)
        R.op("sp", lambda e: e.dma_start(out=xt.v[n16:n], in_=src[r0 + n16:r0 + n, :]), [], [xt.extra[0]], dma=True)

    def store_rows(dst, r0, xo, n, is_out=False):
        n16 = (n // 16) * 16
        parts = [(0, n)] if (n16 == n or n16 == 0) else [(0, n16), (n16, n)]
        for (a, b_) in parts:
            R.op("sp", lambda e, a=a, b_=b_: e.dma_start(out=dst[r0 + a:r0 + b_, :], in_=xo.v[a:b_]),
                 bufs([xo]), [], dma=True, out=is_out)

    def ACT(out, in_, func, reads, writes, **kw):
        R.op("act", lambda e: e.activation(out=out, in_=in_, func=func, **kw), bufs(reads), bufs(writes))

    def TT(eng, out, in0, in1, op, reads, writes):
        R.op(eng, lambda e: e.tensor_tensor(out=out, in0=in0, in1=in1, op=op), bufs(reads), bufs(writes))

    def STT(eng, out, in0, scalar, in1, op0, op1, reads, writes):
        R.op(eng, lambda e: e.scalar_tensor_tensor(out=out, in0=in0, scalar=scalar, in1=in1, op0=op0, op1=op1),
             bufs(reads), bufs(writes))

    def TS(eng, out, in0, s1, s2, op0, op1, reads, writes):
        if s2 is None:
            R.op(eng, lambda e: e.tensor_scalar(out=out, in0=in0, scalar1=s1, scalar2=None, op0=op0),
                 bufs(reads), bufs(writes))
        else:
            R.op(eng, lambda e: e.tensor_scalar(out=out, in0=in0, scalar1=s1, scalar2=s2, op0=op0, op1=op1),
                 bufs(reads), bufs(writes))

    def CP(eng, out, in_, reads, writes):
        R.op(eng, lambda e: e.tensor_copy(out=out, in_=in_), bufs(reads), bufs(writes))

    def MEMSET(eng, out, val, writes):
        R.op(eng, lambda e: e.memset(out, val), (), bufs(writes))

    def DMA(eng, out, in_, reads, writes, is_out=False):
        R.op(eng, lambda e: e.dma_start(out=out, in_=in_), bufs(reads), bufs(writes), dma=True, out=is_out)

    def PE(fn, reads, writes):
        R.op("pe", fn, bufs(reads), bufs(writes))

    ident = ar.alloc([128, 128], BF16, "ident")
    identf = ar.alloc([128, 128], F32, "identf")
    ones = ar.alloc([128, 64], F32, "ones")
    epsD = ar.alloc([128, 1], F32, "epsD")
    gcols = ar.alloc([128, 4, 8], F32, "gcols")
    gqk = ar.alloc([128, 256], F32, "gqk")
    maskt = ar.alloc([128, 2], F32, "mask")
    MEMSET("pool", identf.v, 0.0, [identf])
    R.op("pool", lambda e: e.affine_select(out=identf.v, in_=identf.v, pattern=[[-1, 128]],
                                           compare_op=ALU.not_equal, fill=1.0, base=0, channel_multiplier=1),
         [identf.b], [identf.b])
    CP("pool", ident.v, identf.v, [identf], [ident])
    MEMSET("pool", ones.v, 1.0, [ones])
    onesb = ar.alloc([128, 64], BF16, "onesb")
    MEMSET("pool", onesb.v, 1.0, [onesb])
    MEMSET("pool", epsD.v, EPS, [epsD])
    DMA("sp", gcols.v, gcols_in.rearrange("p (a b) -> p a b", a=4), [], [gcols])
    DMA("sp", gqk.v, gqk_in.partition_broadcast(128), [], [gqk])
    DMA("sp", maskt.v, mask_in, [], [maskt])
    pmark = ar.mark()

    class NormCtx:
        def __init__(self, nslots=2, tbanks=(0,)):
            self.slots = []
            for i in range(nslots):
                xsb = ar.alloc([128, D], BF16, "nxs%d" % i)
                self.slots.append(dict(junk=xsb, xsb=xsb, st=ar.alloc([128, 4], F32, "nst%d" % i)))
            self.i = 0
            self.tbanks = list(tbanks)
            self.scale_eng = "dve"
            self.rsqrt_eng = "act"

    def rsqrt_dve(st, n, scale):
        I32 = mybir.dt.int32
        x, y, t = st.v[0:n, 1:2], st.v[0:n, 2:3], st.v[0:n, 3:4]
        TS("dve", x, st.v[0:n, 0:1], scale, EPS, ALU.mult, ALU.add, [st], [st])
        TS("dve", y.bitcast(I32), x.bitcast(I32), 1, None, ALU.arith_shift_right, None, [st], [st])
        TS("dve", y.bitcast(I32), y.bitcast(I32), -1, 0x5f3759df, ALU.mult, ALU.add, [st], [st])
        for _ in range(3):
            STT("dve", t, y, y, x, ALU.mult, ALU.mult, [st], [st])
            TS("dve", t, t, -0.5, 1.5, ALU.mult, ALU.add, [st], [st])
            TT("dve", y, y, t, ALU.mult, [st], [st])

    def norm_T_ops(nctx, xt, n, gidx, out_ap, out_t):
        s = nctx.slots[nctx.i % len(nctx.slots)]
        tbank = nctx.tbanks[nctx.i % len(nctx.tbanks)]
        nctx.i += 1
        junk, xsb, st = s["junk"], s["xsb"], s["st"]
        tb = PB[tbank]
        pt = bf16_bank(tbank)
        scale_eng = nctx.scale_eng

        def o1():
            R.op("dve", lambda e: e.scalar_tensor_tensor(out=junk.v[0:n], in0=xt.v[0:n], scalar=1.0, in1=xt.v[0:n],
                                                         op0=ALU.mult, op1=ALU.mult, accum_out=st.v[0:n, 0:1]),
                 bufs([xt]), bufs([junk, st]))

        rs_eng = nctx.rsqrt_eng

        def o2():
            if rs_eng == "dve":
                rsqrt_dve(st, n, 1.0 / D)
            else:
                ACT(st.v[0:n, 1:2], st.v[0:n, 0:1], AF.Ln, [st, epsD], [st], scale=1.0 / D, bias=epsD.v[0:n])
                ACT(st.v[0:n, 2:3], st.v[0:n, 1:2], AF.Exp, [st], [st], scale=-0.5)

        def o3():
            if scale_eng == "act":
                ACT(xsb.v[0:n], xt.v[0:n], AF.Copy, [xt, st], [xsb], scale=st.v[0:n, 2:3])
            else:
                TS("dve", xsb.v[0:n], xt.v[0:n], st.v[0:n, 2:3], None, ALU.mult, None, [xt, st], [xsb])

        def o4():
            def tr(e):
                for k in range(8):
                    i = e.transpose(out=pt[:, k, 0:n], in_=xsb.v[0:n, k * 128:(k + 1) * 128], identity=ident.v[0:n, 0:n])
                return i
            PE(tr, [xsb, ident], [tb])

        def o5():
            TT("dve", out_ap, pt[:, :, 0:n], gcols.v[:, gidx, :].unsqueeze(2).broadcast_to([128, 8, n]), ALU.mult,
               [tb, gcols], [out_t])
        return [o1, o2, o3, o4, o5]

    def norm_T(nctx, xt, n, gidx, out_ap, out_t):
        for o in norm_T_ops(nctx, xt, n, gidx, out_ap, out_t):
            o()

    def qk_post_ops(src, H, ropet, goff, scr, out):
        sq, t1, AB, st = scr["sq"], scr["t1"], scr["AB"], scr["st"]
        W = H * 64
        s3 = src.v[:, 0:W].rearrange("p (h d) -> p h d", h=H)

        def o1():
            TT("pool", AB.v, ropet.v, gqk.v[:, goff:goff + 128], ALU.mult, [ropet, gqk], [AB])

        def o2():
            TT("dve", sq.v[:, 0:W], src.v[:, 0:W], src.v[:, 0:W], ALU.mult, [src], [sq])
            R.op("dve", lambda e: e.tensor_reduce(out=st.v[:, 0:H], in_=sq.v[:, 0:W].rearrange("p (h d) -> p h d", h=H),
                                                  axis=AX.X, op=ALU.add), [sq.b], [st.b])
            A_bc = AB.v[:, 0:64].unsqueeze(1).broadcast_to([128, H, 64])
            TT("dve", t1.v[:, 0:W].rearrange("p (h d) -> p h d", h=H), s3, A_bc, ALU.mult, [src, AB], [t1])

        def o3():
            ACT(st.v[:, 32:32 + H], st.v[:, 0:H], AF.Ln, [st, epsD], [st], scale=1.0 / HD, bias=epsD.v)
            ACT(st.v[:, 64:64 + H], st.v[:, 32:32 + H], AF.Exp, [st], [st], scale=-0.5)

        def o4():
            s5 = src.v[:, 0:W].rearrange("p (h a b c) -> p h a b c", h=H, a=2, b=2)
            q5 = sq.v[:, 0:W].rearrange("p (h a b c) -> p h a b c", h=H, a=2, b=2)
            B5 = AB.v[:, 64:128].rearrange("p (a b c) -> p a b c", a=2, b=2)
            for blk in range(2):
                TT("pool", q5[:, :, :, blk, :], s5[:, :, :, 1 - blk, :],
                   B5[:, :, blk, :].unsqueeze(1).broadcast_to([128, H, 2, 16]), ALU.mult, [src, AB, st], [sq])
            TT("pool", t1.v[:, 0:W], t1.v[:, 0:W], sq.v[:, 0:W], ALU.add, [t1, sq], [t1])

        def o5():
            TT("dve", out.v[:, 0:W].rearrange("p (h d) -> p h d", h=H),
               t1.v[:, 0:W].rearrange("p (h d) -> p h d", h=H),
               st.v[:, 64:64 + H].unsqueeze(2).broadcast_to([128, H, 64]), ALU.mult, [t1, st], [out])
        return [o1, o2, o3, o4, o5]

    def qk_post(src, H, ropet, goff, scr, out):
        for o in qk_post_ops(src, H, ropet, goff, scr, out):
            o()

    def pipeline(n_items, make_stages, starts=None):
        if starts is None:
            starts = list(range(n_items))
        live = {}
        nxt = 0
        it_ = 0
        while nxt < n_items or live:
            while nxt < n_items and starts[nxt] <= it_:
                live[nxt] = make_stages(nxt)
                nxt += 1
            for i in sorted(live):
                k = it_ - starts[i]
                if 0 <= k < len(live[i]):
                    live[i][k]()
            for i in [i for i in live if it_ - starts[i] >= len(live[i]) - 1]:
                del live[i]
            it_ += 1

    def interleave(dl, bl):
        if dl:
            dl[0]["load"]()
        if bl:
            bl[0]["load"]()
        for i in range(max(len(dl), len(bl))):
            if i + 1 < len(dl):
                dl[i + 1]["load"]()
            if i < len(bl):
                bl[i]["ops"][0]()
                bl[i]["ops"][1]()
                bl[i]["ops"][2]()
            if i + 1 < len(bl):
                bl[i + 1]["load"]()
            if i < len(dl):
                dl[i]["mm"]()
            if i < len(bl):
                bl[i]["ops"][3]()
            if i < len(dl):
                dl[i]["fin"]()
            if i < len(bl):
                bl[i]["ops"][4]()

    NKT = SEQ_P + SEQ_S
    kT = ar.alloc([128, 2, NKT], BF16, "kT")
    Vx = ar.alloc([128, NKT // 128, 4, 66], BF16, "Vx")
    MEMSET("pool", Vx.v[:, :, :, 64:65], 1.0, [Vx])
    wq = ar.alloc([128, 8, 1024], BF16, "wq")
    DMA("pool", wq.v, w_qkv[:, 0:1024].rearrange("(k p) f -> p k f", p=128), [], [wq])
    abmark = ar.mark()
    wkv = ar.alloc([128, 8, 512], BF16, "wkv")
    DMA("pool", wkv.v, w_qkv[:, 1024:1536].rearrange("(k p) f -> p k f", p=128), [], [wkv])
    nctx = NormCtx(4, tbanks=(0, 5))
    nctx.scale_eng = "act"
    xin = [ar.alloc([128, D], F32, "xin%d" % i) for i in range(4)]
    ropet = [ar.alloc([128, 128], F32, "ropet%d" % i) for i in range(12)]
    xnTa = [ar.alloc([128, 8, 128], BF16, "xnTa%d" % i) for i in range(3)]
    NK = 7
    kraws = [ar.alloc([128, 512], F32, "kraw%d" % i) for i in range(NK)]
    kscrs = [dict(sq=ar.alloc([128, 256], F32, "ksq%d" % i), t1=ar.alloc([128, 256], F32, "kt1%d" % i),
                  AB=ar.alloc([128, 128], F32, "kAB%d" % i), st=ar.alloc([128, 96], F32, "kst%d" % i))
             for i in range(NK)]
    krs = [ar.alloc([128, 256], BF16, "kr%d" % i) for i in range(3)]
    qraws = [ar.alloc([128, 1024], F32, "qraws%d" % i) for i in range(1)]
    qscrs = [dict(sq=ar.alloc([128, 1024], F32, "qssq%d" % i), t1=ar.alloc([128, 1024], F32, "qst1%d" % i),
                  AB=ar.alloc([128, 128], F32, "qsAB%d" % i), st=ar.alloc([128, 96], F32, "qsst%d" % i))
             for i in range(1)]
    qrs = [ar.alloc([128, 1024], BF16, "qrs%d" % i) for i in range(1)]
    qTst = [ar.alloc([128, 8, 128], BF16, "qTst%d" % i) for i in range(2)]
    pQ = T(pview(6, 2), "pQ")

    chunksA = [("P", c) for c in range(SEQ_P // 128)] + [("S", c) for c in range(SEQ_S // 128)]
    if stop_after == "A0":
        chunksA = chunksA[:4]

    def stagesA(ci):
        seg, c = chunksA[ci]
        xsrc = xp_full if seg == "P" else xs_in
        gc = c if seg == "P" else SEQ_P // 128 + c
        xt, rt, xn = xin[ci % 4], ropet[ci % 12], xnTa[ci % 3]
        kraw, kscr, kr = kraws[ci % NK], kscrs[ci % NK], krs[ci % 3]
        kvb = PB[1 + ci % 2]
        tbk = 3 + ci % 2

        def load():
            DMA("sp", xt.v, xsrc[c * 128:(c + 1) * 128, :], [], [xt])
            DMA("sp", rt.v, rope_seq[c * 128:(c + 1) * 128, :], [], [rt])
        n_ops = norm_T_ops(nctx, xt, 128, 0, xn.v, xn)

        def kvmm():
            def mmkv(e):
                for k in range(8):
                    i = e.matmul(kvb.v, lhsT=xn.v[:, k, :], rhs=wkv.v[:, k, :], start=(k == 0), stop=(k == 7))
                return i
            PE(mmkv, [xn, wkv], [kvb])

        def kvcopy():
            ACT(kraw.v, kvb.v, AF.Copy, [kvb], [kraw])
        q_ops = qk_post_ops(kraw, 4, rt, 128, kscr, kr)

        def vcopy_and_q12():
            CP("pool", Vx.v[:, gc, :, 0:64], kraw.v[:, 256:512].rearrange("p (g d) -> p g d", g=4), [kraw], [Vx])
            q_ops[0]()
            q_ops[1]()

        def ktr():
            tb = PB[tbk]
            pt = bf16_bank(tbk)

            def trk(e):
                for j in range(2):
                    i = e.transpose(out=pt[:, j, :], in_=kr.v[:, j * 128:(j + 1) * 128], identity=ident.v)
                return i
            PE(trk, [kr, ident], [tb])

        def kcopy():
            pt = bf16_bank(tbk)
            ACT(kT.v[:, :, gc * 128:(gc + 1) * 128], pt[:, 0:2, :], AF.Copy, [PB[tbk]], [kT])
        stages = [load, n_ops[0], n_ops[1], n_ops[2], n_ops[3], n_ops[4], kvmm, kvcopy, vcopy_and_q12,
                  q_ops[2], q_ops[3], q_ops[4], ktr, kcopy]
        if seg == "S":
            qraw_, qscr_, qr_, qTt = qraws[0], qscrs[0], qrs[0], qTst[c % 2]

            def qmm():
                def mmq(e):
                    for n2 in range(2):
                        for k in range(8):
                            i = e.matmul(pQ.v[:, n2, :], lhsT=xn.v[:, k, :], rhs=wq.v[:, k, n2 * 512:(n2 + 1) * 512],
                                         start=(k == 0), stop=(k == 7))
                    return i
                PE(mmq, [xn, wq], [pQ])

            def qcopy():
                ACT(qraw_.v.rearrange("p (a b) -> p a b", a=2), pQ.v, AF.Copy, [pQ], [qraw_])
            qq = qk_post_ops(qraw_, 16, rt, 0, qscr_, qr_)

            def qtr():
                pt = bf16_bank(6)

                def trq(e):
                    for j in range(8):
                        i = e.transpose(out=pt[:, j, :], in_=qr_.v[:, j * 128:(j + 1) * 128], identity=ident.v)
                    return i
                PE(trq, [qr_, ident], [pQ])

            def qTcopy():
                ACT(qTt.v, bf16_bank(6), AF.Copy, [pQ], [qTt])

            def qstore():
                DMA("sp", QS[c].rearrange("p (a b) -> p a b", a=8), qTt.v, [qTt], [])
            extra = [qmm, qcopy, lambda: (qq[0](), qq[1]()), qq[2], qq[3], qq[4], qtr, qTcopy, qstore]
            for k_, f_ in enumerate(extra):
                idx = 6 + k_
                if idx < len(stages):
                    stages[idx] = (lambda a_=stages[idx], b_=f_: (a_(), b_()))
                else:
                    stages.append(f_)
        return stages

    startsA = []
    t_ = 0
    for (seg_, c_) in chunksA:
        startsA.append(t_)
        t_ += 1 if seg_ == "P" else 4
    pipeline(len(chunksA), stagesA, startsA)
    R.barrier()
    if stop_after in ("A", "A0"):
        dbg_k = nc.dram_tensor("dbg_k", [128, 2 * NKT], BF16, kind="ExternalOutput").ap()
        dbg_v = nc.dram_tensor("dbg_v", [128, (NKT // 128) * 4 * 66], BF16, kind="ExternalOutput").ap()
        DMA("sp", dbg_k.rearrange("p (a b) -> p a b", a=2), kT.v, [kT], [], is_out=True)
        DMA("sp", dbg_v.rearrange("p (a b c) -> p a b c", a=NKT // 128, b=4), Vx.v, [Vx], [], is_out=True)
        with ExitStack() as es_:
            st = R.emit(es_)
        return nc, st

    ar.reset(abmark)
    wo = ar.alloc([128, 16, 1024], BF16, "wo")
    DMA("pool", wo.v[0:64], w_o.rearrange("(s d) f -> d s f", d=64), [], [wo])
    stage_items = []
    if STAGE_W:
        def _st(dst, src):
            return lambda: DMA("pool", dst.rearrange("(k p) f -> p k f", p=128), src.rearrange("(k p) f -> p k f", p=128),
                               [], [])
        for l in range(2):
            for hh in range(2):
                stage_items.append(_st(WUPB[l][hh * 512:(hh + 1) * 512, :], w_up[l, hh * 512:(hh + 1) * 512, :]))
            stage_items.append(_st(WDNB[l], w_dn[l]))
        for hh in range(2):
            stage_items.append(_st(WINB[hh * 512:(hh + 1) * 512, :], w_in[hh * 512:(hh + 1) * 512, :]))
        stage_items.append(_st(WOUTB, w_out))
    from collections import deque
    nctx = NormCtx(2, tbanks=(6,))
    xin = [ar.alloc([128, D], F32, "bxin%d" % i) for i in range(2)]
    xres = [ar.alloc([128, D], F32, "bxres%d" % i) for i in range(2)]
    ropet = [ar.alloc([128, 128], F32, "bropet%d" % i) for i in range(2)]
    xnTb = [ar.alloc([128, 8, 128], BF16, "xnTb%d" % i) for i in range(2)]
    qraw = ar.alloc([128, 1024], F32, "qraw")
    qscr = dict(sq=ar.alloc([128, 1024], F32, "qsq"), t1=ar.alloc([128, 1024], F32, "qt1"),
                AB=ar.alloc([128, 128], F32, "qAB"), st=ar.alloc([128, 96], F32, "qst"))
    qr = ar.alloc([128, 1024], BF16, "qr")
    qTs = [ar.alloc([128, 8, 256], BF16, "qT%d" % i) for i in range(2)]
    PT = [ar.alloc([128, 1024], BF16, "PT%d" % i) for i in range(3)]
    osb = ar.alloc([128, 1024], F32, "osb")
    rrow = ar.alloc([128, 1024], F32, "rrow")
    rhl = ar.alloc([128, 2048], BF16, "rhl")
    OTn = ar.alloc([128, 2, 8, 256], BF16, "OTn")
    pS = [T(pview(0, 2), "pS0"), T(pview(2, 2), "pS1")]
    pO = T(pview(4, 2), "pO")
    B6, B7 = PB[6], PB[7]
    PB4_saved = PB[4]

    tilesB = [("P", t) for t in range(WIN // 256)] + [("S", t) for t in range(SEQ_S // 256)]
    if stop_after == "B0":
        tilesB = [("P", 1), ("P", 2), ("S", 0), ("S", 1)]
    dq = deque()

    def pump(k=1):
        for _ in range(k):
            if dq:
                dq.popleft()()

    def flush():
        while dq:
            dq.popleft()()

    def tile_src(seg):
        return (xp_win if seg == "P" else xs_in), (rope_win if seg == "P" else rope_seq)

    def prologue_items(ti):
        seg, t = tilesB[ti]
        xsrc, rsrc = tile_src(seg)
        qT = qTs[ti % 2]
        items = []
        if seg == "S":
            def ldq():
                for h in range(2):
                    DMA("sp", qT.v[:, :, h * 128:(h + 1) * 128], QS[2 * t + h].rearrange("p (a b) -> p a b", a=8),
                        [], [qT])
            return [ldq]
        for h in range(2):
            r0 = t * 256 + h * 128
            xt, rt, xn = xin[h], ropet[h], xnTb[h]

            def p1(xt=xt, rt=rt, r0=r0):
                DMA("sp", xt.v, xsrc[r0:r0 + 128, :], [], [xt])
                DMA("sp", rt.v, rsrc[r0:r0 + 128, :], [], [rt])
            items.append(p1)
            def nrm(xt=xt, xn=xn):
                o = norm_T_ops(nctx, xt, 128, 0, xn.v, xn)
                dq_bg.extendleft(reversed([o[0], o[1], o[2], lambda: (o[3](), o[4]())]))
            items.append(nrm)

            def mm(n2, xn=xn):
                def mmq(e):
                    for k in range(8):
                        i = e.matmul(B7.v, lhsT=xn.v[:, k, :], rhs=wq.v[:, k, n2 * 512:(n2 + 1) * 512],
                                     start=(k == 0), stop=(k == 7))
                    return i
                PE(mmq, [xn, wq], [B7])

            def cp(n2):
                CP("dve", qraw.v[:, n2 * 512:(n2 + 1) * 512], B7.v, [B7], [qraw])
            items += [lambda mm=mm, cp=cp: (mm(0), cp(0)), lambda mm=mm, cp=cp: (mm(1), cp(1))]
            qo = qk_post_ops(qraw, 16, rt, 0, qscr, qr)
            items += [lambda qo=qo: (qo[0](), qo[1]()), qo[2], qo[3], qo[4]]

            def tr():
                pt = bf16_bank(6)

                def trq(e):
                    for j in range(8):
                        i = e.transpose(out=pt[:, j, :], in_=qr.v[:, j * 128:(j + 1) * 128], identity=ident.v)
                    return i
                PE(trq, [qr, ident], [B6])

            def cpq(h=h):
                CP("dve", qT.v[:, :, h * 128:(h + 1) * 128], bf16_bank(6), [B6], [qT])
            items += [lambda tr=tr, cpq=cpq: (tr(), cpq())]
        return items

    def unit_epilogue_items(pi0):
        def g1():
            ACT(rrow.v[64:65, :], osb.v[64:65, :], AF.Ln, [osb], [rrow])
            ACT(rrow.v[64:65, :], rrow.v[64:65, :], AF.Exp, [rrow], [rrow], scale=-1.0)
            CP("dve", rhl.v[64:65, 0:1024], rrow.v[64:65, :], [rrow], [rhl])
            TT("dve", rhl.v[64:65, 1024:2048], rrow.v[64:65, :], rhl.v[64:65, 0:1024], ALU.subtract, [rrow, rhl], [rhl])

        def g2():
            for par, bk in ((0, B6), (1, B7)):
                def bc(e, par=par, bk=bk):
                    e.matmul(bk.v[0:64, :], lhsT=onesb.v[64:65, :], rhs=rhl.v[64:65, par * 512:(par + 1) * 512],
                             start=True, stop=False)
                    return e.matmul(bk.v[0:64, :], lhsT=onesb.v[64:65, :],
                                    rhs=rhl.v[64:65, 1024 + par * 512:1024 + (par + 1) * 512], start=False, stop=True)
                PE(bc, [rhl, onesb], [bk])

        def g3():
            for par, bk in ((0, B6), (1, B7)):
                TT("dve", OTn.v[0:64, par, pi0:pi0 + 2, :],
                   osb.v[0:64, par * 512:(par + 1) * 512].rearrange("p (a t) -> p a t", a=2),
                   bk.v[0:64, :].rearrange("p (a t) -> p a t", a=2), ALU.mult, [osb, bk], [OTn])
        return [g1, lambda: (g2(), g3())]

    def tile_epilogue_items(ti):
        seg, t = tilesB[ti]
        base, _ = SEG[seg]
        xsrc, _r = tile_src(seg)
        items = []
        for h in range(2):
            r0 = t * 256 + h * 128
            xt = xres[h]

            def w1(xt=xt, r0=r0):
                DMA("sp", xt.v, xsrc[r0:r0 + 128, :], [], [xt])

            def wmm(n2, h=h):
                bk = B6 if n2 == 0 else B7

                def mmo(e):
                    for s_ in range(16):
                        i = e.matmul(bk.v, lhsT=OTn.v[0:64, s_ % 2, s_ // 2, h * 128:(h + 1) * 128],
                                     rhs=wo.v[0:64, s_, n2 * 512:(n2 + 1) * 512], start=(s_ == 0), stop=(s_ == 15))
                    return i
                PE(mmo, [OTn, wo], [bk])

            def wadd(n2, xt=xt):
                bk = B6 if n2 == 0 else B7
                TT("dve", xt.v[:, n2 * 512:(n2 + 1) * 512], bk.v, xt.v[:, n2 * 512:(n2 + 1) * 512], ALU.add,
                   [bk, xt], [xt])

            def w3(xt=xt, r0=r0):
                DMA("sp", X1[base + r0:base + r0 + 128, :], xt.v, [xt], [])
            items += [w1, lambda wmm=wmm, wadd=wadd: (wmm(0), wadd(0)), lambda wmm=wmm, wadd=wadd: (wmm(1), wadd(1)), w3]
        return items

    dq_urgent = deque()
    dq_bg = deque()

    def pump(k=1):
        for _ in range(k):
            if dq_urgent:
                dq_urgent.popleft()()
            elif dq_bg:
                dq_bg.popleft()()

    def flush_urgent():
        while dq_urgent:
            dq_urgent.popleft()()

    def flush_all():
        while dq_urgent or dq_bg:
            pump()

    sidx = 0
    dq_bg.extend(prologue_items(0))
    flush_all()
    for ti, (seg, t) in enumerate(tilesB):
        nk = (SEQ_P if seg == "P" else SEQ_S) // 128
        kbase = 0 if seg == "P" else SEQ_P
        qT = qTs[ti % 2]
        if ti + 1 < len(tilesB):
            dq_bg.extend(prologue_items(ti + 1))
        if stage_items:
            dq_bg.append(stage_items.pop(0))
        units = [(gp, j0) for gp in range(2) for j0 in (0, 2)]
        for (gp, j0) in units:
            pi0 = 4 * gp + j0

            def S_mm(e, kc, slot, gp=gp, pi0=pi0, kbase=kbase, qT=qT):
                k0 = kbase + kc * 128
                e.matmul(pS[slot].v[:, 0, :], lhsT=kT.v[0:64, gp, k0:k0 + 128], rhs=qT.v[0:64, pi0:pi0 + 2, :],
                         start=True, stop=True)
                return e.matmul(pS[slot].v[:, 1, :], lhsT=kT.v[64:128, gp, k0:k0 + 128],
                                rhs=qT.v[64:128, pi0:pi0 + 2, :], start=True, stop=True)

            def PV_mm(e, kc, pt, gp=gp, nk=nk, kbase=kbase):
                gck = (kbase // 128) + kc
                e.matmul(pO.v[0:65, 0, :], lhsT=Vx.v[:, gck, 2 * gp, 0:65], rhs=pt.v[:, 0:512],
                         start=(kc == 0), stop=(kc == nk - 1))
                return e.matmul(pO.v[0:65, 1, :], lhsT=Vx.v[:, gck, 2 * gp + 1, 0:65], rhs=pt.v[:, 512:1024],
                                start=(kc == 0), stop=(kc == nk - 1))

            pend = None
            for kc in range(nk):
                slot = sidx % 2
                ptile = PT[sidx % 3]
                sidx += 1
                PE(lambda e, f=S_mm, kc=kc, slot=slot: f(e, kc, slot), [kT, qT], [pS[slot]])
                ACT(ptile.v.rearrange("p (a b) -> p a b", a=2), pS[slot].v, AF.Exp, [pS[slot]], [ptile],
                    scale=1.0 / math.sqrt(HD))
                if pend is not None:
                    PE(lambda e, f=PV_mm, a=pend: f(e, a[0], a[1]), [Vx, pend[1]], [pO])
                pend = (kc, ptile)
                if kc >= 1 and (nk <= 16 or kc % 2 == 0):
                    pump()
            PE(lambda e, f=PV_mm, a=pend: f(e, a[0], a[1]), [Vx, pend[1]], [pO])
            flush_urgent()
            CP("dve", osb.v[0:65].rearrange("p (a b) -> p a b", a=2), pO.v[0:65], [pO], [osb])
            dq_urgent.extend(unit_epilogue_items(pi0))
        dq_urgent.extend(tile_epilogue_items(ti))
        flush_all() if ti + 1 == len(tilesB) else None
        while dq_bg:
            dq_bg.popleft()()
    while stage_items:
        stage_items.pop(0)()
    R.barrier()
    PB[4] = PB4_saved
    if stop_after in ("B", "B0"):
        dbg = nc.dram_tensor("dbg_x1", [NROW, D], F32, kind="ExternalOutput").ap()
        DMA("sp", dbg, X1, [], [], is_out=True)
        with ExitStack() as es_:
            st = R.emit(es_)
        return nc, st

    def ffn_phase(layer, Xin, jobs):
        ar.reset(pmark)
        wup = ar.alloc([128, 8, 2 * FF], BF16, "wup")
        wdn = ar.alloc([128, 22, D], BF16, "wdn")
        cw = ar.alloc([128, 44, 3], F32, "cw")
        cb = ar.alloc([128, 44], F32, "cb")
        wupb = [Buf("wup%d" % k) for k in range(8)]
        wq_eng = "sp" if STAGE_W else "pool"
        wsrc_up = WUPB[layer] if STAGE_W else w_up[layer]
        wsrc_dn = WDNB[layer] if STAGE_W else w_dn[layer]
        for k in range(8):
            R.op(wq_eng, lambda e, k=k: e.dma_start(out=wup.v[:, k, :], in_=wsrc_up[k * 128:(k + 1) * 128, :]),
                 [], [wupb[k]], dma=True)
        for hh in range(2):
            R.op(wq_eng, lambda e, hh=hh: e.dma_start(
                out=wdn.v[:, hh * 11:(hh + 1) * 11, :],
                in_=wsrc_dn[hh * 11 * 128:(hh + 1) * 11 * 128, :].rearrange("(i p) f -> p i f", p=128)),
                [], [wdn.b], dma=True)
        DMA("sp", cw.v, convw_in[layer].rearrange("p (a b) -> p a b", a=44), [], [cw])
        DMA("sp", cb.v, convb_in[layer], [], [cb])
        nctx = NormCtx(2)
        nctx.scale_eng = "act"
        nctx.tbanks = [0]
        xnTs = [ar.alloc([128, 8, 512], BF16, "fxnT%d" % i) for i in range(2)]
        gT = ar.alloc([128, 22, 512], BF16, "fgT")
        tg = [ar.alloc([128, 512], F32, "ftg%d" % i) for i in range(2)]
        tu = [ar.alloc([128, 512], F32, "ftu%d" % i) for i in range(2)]
        sg = [ar.alloc([128, 512], F32, "fsg%d" % i) for i in range(2)]
        gidx = 1 if layer == 0 else 3
        cnt = dict(xi=0, pidx=0)
        tiles = []
        for (seg, s, e_, outd, orow) in jobs:
            for (ts_, n) in split(e_ - s, 510):
                tiles.append(dict(seg=seg, t0=s + ts_, n=n, outd=outd, orow=orow))

        xb = [ar.alloc([128, D], F32, "fxb%d" % i) for i in range(2)]
        xd = [ar.alloc([128, D], F32, "fxd%d" % i) for i in range(2)]

        def build_items(ti):
            tl = tiles[ti]
            seg, t0, n = tl["seg"], tl["t0"], tl["n"]
            base, slen = SEG[seg]
            xnT = xnTs[ti % 2]
            ncol = n + 2
            c_lo = t0 - 1
            lo = max(c_lo, 0)
            hi = min(c_lo + ncol, slen)

            def edges():
                if c_lo < 0:
                    MEMSET("pool", xnT.v[:, :, 0:1], 0.0, [xnT])
                if c_lo + ncol > slen:
                    MEMSET("pool", xnT.v[:, :, ncol - 1:ncol], 0.0, [xnT])
            items = []
            for (ss, sn) in split(hi - lo, 128):
                r0 = lo + ss
                col = r0 - c_lo
                xt = xb[cnt["xb"] % 2]
                cnt["xb"] += 1
                ops = norm_T_ops(nctx, xt, sn, gidx, xnT.v[:, :, col:col + sn], xnT)
                items.append(dict(load=(lambda xt=xt, r0=r0, sn=sn: load_rows(xt, Xin, base + r0, sn)), ops=ops))

            def masks():
                if seg == "P":
                    for mi, mcol in ((0, HALO - 1), (1, WIN - HALO)):
                        if c_lo <= mcol < c_lo + ncol:
                            cc = mcol - c_lo
                            TS("dve", xnT.v[:, :, cc:cc + 1], xnT.v[:, :, cc:cc + 1], maskt.v[:, mi:mi + 1], None,
                               ALU.mult, None, [xnT, maskt], [xnT])
            return edges, items, masks

        def up_loop(ti):
            tl = tiles[ti]
            n = tl["n"]
            ncol = n + 2
            xnT = xnTs[ti % 2]
            prev_gate = None
            for i in range(22):
                pidx = cnt["pidx"]
                cnt["pidx"] += 1
                pg = PB[1 + 2 * (pidx % 2)]
                pu = PB[2 + 2 * (pidx % 2)]
                sl = pidx % 2

                def mmup(e, i=i, pg=pg, pu=pu, ncol=ncol, xnT=xnT):
                    for (pp, f0) in ((pg, i * 128), (pu, FF + i * 128)):
                        for k in range(8):
                            ins = e.matmul(pp.v[:, 0:ncol], lhsT=wup.v[:, k, f0:f0 + 128], rhs=xnT.v[:, k, 0:ncol],
                                           start=(k == 0), stop=(k == 7))
                    return ins
                PE(mmup, [xnT] + wupb, [pg, pu])
                for (pp, tt_, ci) in ((pg, tg[sl], i), (pu, tu[sl], 22 + i)):
                    ACT(tt_.v[:, 0:n], pp.v[:, 1:n + 1], AF.Identity, [pp, cw, cb], [tt_],
                        scale=cw.v[:, ci, 1:2], bias=cb.v[:, ci:ci + 1])
                for (pp, tt_, ci) in ((pg, tg[sl], i), (pu, tu[sl], 22 + i)):
                    STT("dve", tt_.v[:, 0:n], pp.v[:, 0:n], cw.v[:, ci, 0:1], tt_.v[:, 0:n], ALU.mult, ALU.add,
                        [pp, cw, tt_], [tt_])
                    STT("dve", tt_.v[:, 0:n], pp.v[:, 2:n + 2], cw.v[:, ci, 2:3], tt_.v[:, 0:n], ALU.mult, ALU.add,
                        [pp, cw, tt_], [tt_])

                def gate(i=i, sl=sl):
                    ACT(sg[sl].v[:, 0:n], tg[sl].v[:, 0:n], AF.Silu, [tg[sl]], [sg[sl]])
                    TT("pool", gT.v[:, i, 0:n], sg[sl].v[:, 0:n], tu[sl].v[:, 0:n], ALU.mult, [sg[sl], tu[sl]], [gT])
                if prev_gate is not None:
                    prev_gate()
                prev_gate = gate
            prev_gate()

        def down_items(ti):
            tl = tiles[ti]
            seg, t0, n, outd, orow = tl["seg"], tl["t0"], tl["n"], tl["outd"], tl["orow"]
            base, slen = SEG[seg]
            items = []
            for (ss, sn) in split(n, 128):
                r0 = t0 + ss
                xt = xd[cnt["xd"] % 2]
                cnt["xd"] += 1
                pdb = [PB[5], PB[6]]

                def mm(ss=ss, sn=sn, pdb=pdb):
                    def mmdn(e):
                        for n2 in range(2):
                            for i in range(22):
                                ins = e.matmul(psum[0:sn, 5 + n2, :], lhsT=gT.v[:, i, ss:ss + sn],
                                               rhs=wdn.v[:, i, n2 * 512:(n2 + 1) * 512], start=(i == 0), stop=(i == 21))
                        return ins
                    PE(mmdn, [gT, wdn], pdb)

                def fin(xt=xt, sn=sn, r0=r0, pdb=pdb):
                    TT("dve", xt.v[0:sn].rearrange("p (a b) -> p a b", a=2), psum[0:sn, 5:7, :],
                       xt.v[0:sn].rearrange("p (a b) -> p a b", a=2), ALU.add, pdb + [xt], [xt])
                    store_rows(outd, r0 + orow, xt, sn, is_out=True)
                items.append(dict(load=(lambda xt=xt, r0=r0, sn=sn: load_rows(xt, Xin, base + r0, sn)), mm=mm, fin=fin))
            return items

        cnt["xb"] = 0
        cnt["xd"] = 0
        pre, bl, post = build_items(0)
        pre()
        interleave([], bl)
        post()
        for ti in range(len(tiles)):
            up_loop(ti)
            dl = down_items(ti)
            if ti + 1 < len(tiles):
                pre, bl, post = build_items(ti + 1)
                pre()
                interleave(dl, bl)
                post()
            else:
                interleave(dl, [])
        R.barrier()

    ffn_phase(0, X1, [("P", 128, WIN - 128, X2, 0), ("S", 0, SEQ_S, X2, WIN)])
    if stop_after == "C":
        dbg = nc.dram_tensor("dbg_x2", [NROW, D], F32, kind="ExternalOutput").ap()
        DMA("sp", dbg, X2, [], [], is_out=True)
        with ExitStack() as es_:
            st = R.emit(es_)
        return nc, st

    ar.reset(pmark)
    win = ar.alloc([128, 8, 4096], BF16, "win")
    wout = ar.alloc([128, 16, D], BF16, "wout")
    wsT = ar.alloc([128, 8, 128], BF16, "wsT")
    vgain = ar.alloc([128, 2048], F32, "vgain")
    bsT = ar.alloc([128, 16, 128], F32, "bsT")
    winb = [Buf("win%d" % k) for k in range(8)]
    wq_eng = "sp" if STAGE_W else "pool"
    for k in range(8):
        R.op(wq_eng, lambda e, k=k: e.dma_start(out=win.v[:, k, :], in_=(WINB if STAGE_W else w_in)[k * 128:(k + 1) * 128, :]),
             [], [winb[k]], dma=True)
    DMA(wq_eng, wout.v, (WOUTB if STAGE_W else w_out).rearrange("(i p) f -> p i f", p=128), [], [wout])
    DMA("pool", wsT.v, ws_T.rearrange("p (g q) -> p g q", g=8), [], [wsT])
    DMA("sp", vgain.v, vgain_in.partition_broadcast(128), [], [vgain])
    DMA("sp", bsT.v, bsT_in.partition_broadcast(128).rearrange("p (a b) -> p a b", a=16), [], [bsT])
    nctx = NormCtx(2)
    nctx.tbanks = [0]
    nctx.rsqrt_eng = "dve"
    xnTs = [ar.alloc([128, 8, 512], BF16, "dxnT%d" % i) for i in range(2)]
    uT = ar.alloc([128, 16, 512], BF16, "duT")
    yT = ar.alloc([128, 16, 512], BF16, "dyT")
    vgs = [ar.alloc([128, 2048], F32, "dvg%d" % i) for i in range(2)]
    vhats = [ar.alloc([128, 2048], BF16, "dvhat%d" % i) for i in range(2)]
    vsts = [ar.alloc([128, 4], F32, "dvst%d" % i) for i in range(2)]
    tsb = [ar.alloc([128, 512], F32, "dts%d" % i) for i in range(1)]
    cntD = dict(xi=0, ui=0, vi=0, si=0, ci=0)
    tilesD = []
    for (seg, c0, c1) in [("P", 1, 19), ("S", 0, 16)]:
        for (ts_, ncn) in split(c1 - c0, 4):
            tilesD.append(dict(seg=seg, cs=c0 + ts_, ncn=ncn))

    xbD = [ar.alloc([128, D], F32, "dxb%d" % i) for i in range(1)]
    xdD = [ar.alloc([128, D], F32, "dxd%d" % i) for i in range(2)]
    cntD["xb"] = 0
    cntD["xd"] = 0

    def D_build_items(ti):
        tl = tilesD[ti]
        base, slen = SEG[tl["seg"]]
        xnT = xnTs[ti % 2]
        items = []
        for j in range(tl["ncn"]):
            r0 = (tl["cs"] + j) * 128
            xt = xbD[0]
            cntD["xb"] += 1
            ops = norm_T_ops(nctx, xt, 128, 2, xnT.v[:, :, j * 128:(j + 1) * 128], xnT)
            items.append(dict(load=(lambda xt=xt, r0=r0: DMA("sp", xt.v, X2[base + r0:base + r0 + 128, :], [], [xt])),
                              ops=ops))
        return items

    def D_U(ti):
        tl = tilesD[ti]
        ncol = tl["ncn"] * 128
        xnT = xnTs[ti % 2]
        for fc in range(16):
            pb_ = PB[1 + cntD["ui"] % 2]
            cntD["ui"] += 1

            def mmu(e, fc=fc, pb_=pb_):
                for k in range(8):
                    ins = e.matmul(pb_.v[:, 0:ncol], lhsT=win.v[:, k, fc * 128:(fc + 1) * 128], rhs=xnT.v[:, k, 0:ncol],
                                   start=(k == 0), stop=(k == 7))
                return ins
            PE(mmu, [xnT] + winb, [pb_])
            ACT(uT.v[:, fc, 0:ncol], pb_.v[:, 0:ncol], AF.Gelu_apprx_tanh, [pb_], [uT])

    def D_V(ti, j, ci):
        xnT = xnTs[ti % 2]
        vg = vgs[ci % 2]
        for qd in range(4):
            pb_ = PB[3 + cntD["vi"] % 2]
            cntD["vi"] += 1

            def mmv(e, qd=qd, pb_=pb_):
                for k in range(8):
                    ins = e.matmul(pb_.v, lhsT=xnT.v[:, k, j * 128:(j + 1) * 128],
                                   rhs=win.v[:, k, 2048 + qd * 512:2048 + (qd + 1) * 512],
                                   start=(k == 0), stop=(k == 7))
                return ins
            PE(mmv, [xnT] + winb, [pb_])
            ACT(vg.v[:, qd * 512:(qd + 1) * 512], pb_.v, AF.Gelu_apprx_tanh, [pb_], [vg])

    def D_N(ci):
        vhat, vst = vhats[ci % 2], vsts[ci % 2]
        vg = vgs[ci % 2]
        ACT(vhat.v, vg.v, AF.Square, [vg], [vhat, vst], accum_out=vst.v[:, 0:1])
        rsqrt_dve(vst, 128, 1.0 / 2048)
        STT("dve", vhat.v, vg.v, vst.v[:, 2:3], vgain.v, ALU.mult, ALU.mult, [vg, vst, vgain], [vhat])

    def D_S(ci, j):
        vhat = vhats[ci % 2]
        for q4 in range(4):
            pb_ = PB[5 + cntD["si"] % 2]
            tt_ = tsb[0]
            cntD["si"] += 1

            def mms(e, q4=q4, pb_=pb_):
                for f in range(4):
                    fc = q4 * 4 + f
                    ins = e.matmul(pb_.v[:, f * 128:(f + 1) * 128], lhsT=vhat.v[:, fc * 128:(fc + 1) * 128],
                                   rhs=wsT.v[:, fc // 2, :], start=True, stop=True)
                return ins
            PE(mms, [vhat, wsT], [pb_])
            TT("dve", tt_.v.rearrange("p (a b) -> p a b", a=4), pb_.v.rearrange("p (a b) -> p a b", a=4),
               bsT.v[:, q4 * 4:(q4 + 1) * 4, :], ALU.add, [pb_, bsT], [tt_])
            TT("pool", yT.v[:, q4 * 4:(q4 + 1) * 4, j * 128:(j + 1) * 128],
               tt_.v.rearrange("p (a b) -> p a b", a=4), uT.v[:, q4 * 4:(q4 + 1) * 4, j * 128:(j + 1) * 128],
               ALU.mult, [tt_, uT], [yT])

    def D_O_items(ti):
        tl = tilesD[ti]
        base, slen = SEG[tl["seg"]]
        items = []
        for j in range(tl["ncn"]):
            r0 = (tl["cs"] + j) * 128
            xt = xdD[cntD["xd"] % 2]
            cntD["xd"] += 1
            pdb = [PB[1], PB[2]]

            def mm(j=j, pdb=pdb):
                def mmout(e):
                    for n2 in range(2):
                        for fc in range(16):
                            ins = e.matmul(psum[:, 1 + n2, :], lhsT=yT.v[:, fc, j * 128:(j + 1) * 128],
                                           rhs=wout.v[:, fc, n2 * 512:(n2 + 1) * 512], start=(fc == 0), stop=(fc == 15))
                    return ins
                PE(mmout, [yT, wout], pdb)

            def fin(xt=xt, r0=r0, pdb=pdb):
                TT("dve", xt.v.rearrange("p (a b) -> p a b", a=2), psum[:, 1:3, :],
                   xt.v.rearrange("p (a b) -> p a b", a=2), ALU.add, pdb + [xt], [xt])
                DMA("sp", X3[base + r0:base + r0 + 128, :], xt.v, [xt], [])
            items.append(dict(load=(lambda xt=xt, r0=r0: DMA("sp", xt.v, X2[base + r0:base + r0 + 128, :], [], [xt])),
                              mm=mm, fin=fin))
        return items

    interleave([], D_build_items(0))
    for ti in range(len(tilesD)):
        ncn = tilesD[ti]["ncn"]
        D_U(ti)
        pendS = None
        for j in range(ncn):
            ci = cntD["ci"]
            cntD["ci"] += 1
            D_V(ti, j, ci)
            D_N(ci)
            if pendS is not None:
                D_S(*pendS)
            pendS = (ci, j)
        D_S(*pendS)
        interleave(D_O_items(ti), D_build_items(ti + 1) if ti + 1 < len(tilesD) else [])
    R.barrier()
    if stop_after == "D":
        dbg = nc.dram_tensor("dbg_x3", [NROW, D], F32, kind="ExternalOutput").ap()
        DMA("sp", dbg, X3, [], [], is_out=True)
        with ExitStack() as es_:
            st = R.emit(es_)
        return nc, st

    ffn_phase(1, X3, [("P", HALO, WIN - HALO, y_p, -HALO), ("S", 0, SEQ_S, y_s, 0)])
    with ExitStack() as es_:
        st = R.emit(es_)
    return nc, st


def rope_table(pos):
    pos = np.asarray(pos, dtype=np.int64)
    row = (pos // 64).astype(np.float32)
    col = (pos % 64).astype(np.float32)
    inv = (10000.0 ** (-np.arange(0, 32, 2, dtype=np.float32) / np.float32(32))).astype(np.float32)
    ar_ = row[:, None] * inv[None, :]
    ac_ = col[:, None] * inv[None, :]
    ang = np.concatenate([ar_, ar_, ac_, ac_], axis=-1).astype(np.float32)
    cos = np.cos(ang).astype(np.float32)
    sin = np.sin(ang).astype(np.float32)
    sgn = np.concatenate([-np.ones(16), np.ones(16), -np.ones(16), np.ones(16)]).astype(np.float32)
    return np.ascontiguousarray(np.concatenate([cos, sin * sgn[None, :]], axis=-1).astype(np.float32))


def swap16(g):
    g = np.asarray(g, dtype=np.float32).reshape(2, 2, 16)
    return np.ascontiguousarray(g[:, ::-1, :]).reshape(64)


def prep_inputs(inp):
    f = lambda a: np.ascontiguousarray(np.asarray(a, dtype=np.float32))
    x_prompt, x_sample = f(inp["x_prompt"]), f(inp["x_sample"])
    wqkv = f(inp["attn_w_qkv"])[0]
    perm = np.concatenate([np.arange(head_of_slot(s) * 64, head_of_slot(s) * 64 + 64) for s in range(16)])
    wqkv_p = np.ascontiguousarray(np.concatenate([wqkv[:, perm], wqkv[:, 1024:]], axis=1))
    wo_p = np.ascontiguousarray(f(inp["attn_w_o"])[0][perm, :])
    gq, gk = f(inp["attn_q_norm"])[0], f(inp["attn_k_norm"])[0]
    gqk = np.ascontiguousarray(np.concatenate([gq, swap16(gq), gk, swap16(gk)]).astype(np.float32))
    nm, nf = f(inp["norm_mix"]), f(inp["norm_ffn"])
    gcols = np.stack([nm[0], nf[0], nm[1], nf[1]], 0).reshape(4, 8, 128).transpose(2, 0, 1).reshape(128, 32)
    ws = f(inp["sgu_w_s"])[0]
    ws_T = np.ascontiguousarray(ws.transpose(2, 0, 1)).reshape(128, 8 * 128)
    bs = f(inp["sgu_b_s"])[0]
    bsT = np.ascontiguousarray(np.repeat(bs, 2, axis=0)).reshape(2048)
    cwt = f(inp["ffn_conv_w"])
    convw = np.ascontiguousarray(cwt.reshape(2, 3, 44, 128).transpose(0, 3, 2, 1)).reshape(2, 128, 132)
    convb = np.ascontiguousarray(f(inp["ffn_conv_b"]).reshape(2, 44, 128).transpose(0, 2, 1))
    shared = dict(
        rope_seq=rope_table(np.arange(SEQ_P)), w_qkv=wqkv_p, w_o=wo_p, w_in=f(inp["sgu_w_in"])[0],
        w_out=f(inp["sgu_w_out"])[0], ws_T=ws_T, w_up=f(inp["ffn_w_up"]), w_dn=f(inp["ffn_w_down"]),
        gcols=np.ascontiguousarray(gcols.astype(np.float32)), gqk=gqk, vgain=f(inp["sgu_v_norm"]).reshape(2048),
        bsT=bsT, convw=convw, convb=convb)
    maps = []
    for c in range(8):
        b, q = c // 4, c % 4
        a0 = 2048 * q - HALO
        win = np.zeros((WIN, D), np.float32)
        lo, hi = max(a0, 0), min(a0 + WIN, SEQ_P)
        win[lo - a0:hi - a0] = x_prompt[b, lo:hi]
        pos = np.clip(np.arange(a0, a0 + WIN), 0, SEQ_P - 1)
        mask = np.zeros((128, 2), np.float32)
        mask[:, 0] = 0.0 if q == 0 else 1.0
        mask[:, 1] = 0.0 if q == 3 else 1.0
        m = dict(shared)
        m.update(xp_full=x_prompt[b], xp_win=win, xs=x_sample[c], rope_win=rope_table(pos), mask=mask)
        maps.append(m)
    return maps


_NC_CACHE = {}


def kernel(**inputs):
    maps = prep_inputs(inputs)
    if "nc" not in _NC_CACHE:
        _NC_CACHE["nc"] = build()[0]
    nc = _NC_CACHE["nc"]
    res = run_bass_kernel_spmd(nc, maps, core_ids=list(range(8)))
    yp = np.zeros((2, SEQ_P, D), np.float32)
    ys = np.zeros((8, SEQ_S, D), np.float32)
    for c in range(8):
        r = res.results[c]
        yp[c // 4, 2048 * (c % 4):2048 * (c % 4 + 1)] = np.asarray(r["y_p"], dtype=np.float32)
        ys[c] = np.asarray(r["y_s"], dtype=np.float32)
    return yp, ys
```
